# Optimizing a Trainium2 kernel written in Bass

```python
import jax, jax.numpy as jnp
from jax import lax
import numpy as np

D_MODEL = 1024
BATCH = 8
SEQ = 2048
DEPTH = 4
DEC_BATCH = 128
DEC_SEQ = 1
PAST_LEN = 16384
PAGE_SIZE = 128

W_A = D_MODEL
W_B = D_MODEL
W_C = 2 * D_MODEL
POOL_WINDOWS = (2, 4, 8, 16)
POOL_GROUPS = 4
POOL_GW = W_A // POOL_GROUPS
POOL_CTX = 15
SGU_GROUPS = 4
SGU_GW = W_B // SGU_GROUPS
CHUNK = 128
CONV_WIDTH = 3
CONV_CTX = CONV_WIDTH - 1
N_MEM = 256
XA_HEADS = 4
XA_HD = D_MODEL // XA_HEADS
EPS = 1e-6

kernel_name = "hybrid_pool_sgu_conv_memxattn_step"


def rms_norm(x, g):
    xf = x.astype(jnp.float32)
    y = xf * lax.rsqrt(jnp.mean(xf * xf, axis=-1, keepdims=True) + EPS)
    return (y * g.astype(jnp.float32)).astype(x.dtype)


def causal_multi_pool(ext, n_new, pos0):
    n_ctx = ext.shape[1] - n_new
    cs = jnp.cumsum(ext.astype(jnp.float32), axis=1)
    cs = jnp.pad(cs, ((0, 0), (1, 0), (0, 0)))
    pos = pos0 + jnp.arange(n_new)
    outs = []
    for g, w in enumerate(POOL_WINDOWS):
        sl = slice(g * POOL_GW, (g + 1) * POOL_GW)
        hi = cs[:, n_ctx + 1:n_ctx + 1 + n_new, sl]
        lo = cs[:, n_ctx + 1 - w:n_ctx + 1 - w + n_new, sl]
        cnt = jnp.minimum(pos + 1, w).astype(jnp.float32)[None, :, None]
        outs.append((hi - lo) / cnt)
    mean = jnp.concatenate(outs, axis=-1)
    return (mean - ext[:, n_ctx:].astype(jnp.float32)).astype(ext.dtype)


def chunk_spatial_mix(v, w_s, b_s):
    b, L, _ = v.shape
    n_chunks = -(-L // CHUNK)
    lp = n_chunks * CHUNK
    vp = jnp.pad(v, ((0, 0), (0, lp - L), (0, 0))).reshape(b, n_chunks, CHUNK, SGU_GROUPS, SGU_GW)
    mask = jnp.tril(jnp.ones((CHUNK, CHUNK), dtype=bool))
    w = jnp.where(mask[None], w_s, jnp.zeros_like(w_s))
    mixed = jnp.einsum('gts,bnsgc->bntgc', w, vp) + b_s.T[None, None, :, :, None]
    return mixed.reshape(b, lp, W_B)[:, :L]


def pool_sgu_mixer(h, pool_buf, pos0, w_in, pool_maps, pool_scale, sgu_w, sgu_b, sgu_g, w_out):
    b, L, _ = h.shape
    z = h @ w_in
    xa = z[..., :W_A]
    ga = z[..., W_A:2 * W_A]
    u = z[..., 2 * W_A:2 * W_A + W_B]
    v = z[..., 2 * W_A + W_B:2 * W_A + 2 * W_B]
    gb = z[..., 2 * W_A + 2 * W_B:]
    ext = jnp.concatenate([pool_buf, xa], axis=1)
    pooled = causal_multi_pool(ext, L, pos0).reshape(b, L, POOL_GROUPS, POOL_GW)
    ya = jnp.einsum('blgc,gcd->blgd', pooled, pool_maps).reshape(b, L, W_A) * pool_scale
    ya = ya * jax.nn.silu(ga)
    vn = rms_norm(v, sgu_g)
    yb = u * chunk_spatial_mix(vn, sgu_w, sgu_b) * jax.nn.silu(gb)
    out = jnp.concatenate([ya, yb], axis=-1) @ w_out
    return out, ext[:, -POOL_CTX:], vn


def short_conv_mixer(h, conv_buf, w_in, conv_w, w_out):
    L = h.shape[1]
    z = h @ w_in
    bg = z[..., :W_C]
    cg = z[..., W_C:2 * W_C]
    xc = z[..., 2 * W_C:3 * W_C]
    g = z[..., 3 * W_C:]
    ext = jnp.concatenate([conv_buf, cg * xc], axis=1)
    y = conv_w[0] * ext[:, 0:L] + conv_w[1] * ext[:, 1:L + 1] + conv_w[2] * ext[:, 2:L + 2]
    out = (bg * y * jax.nn.silu(g)) @ w_out
    return out, ext[:, -CONV_CTX:]


def memory_kv(mem, g, w_k, w_v):
    b, m, _ = mem.shape
    mn = rms_norm(mem, g)
    k = (mn @ w_k).reshape(b, m, XA_HEADS, XA_HD)
    v = (mn @ w_v).reshape(b, m, XA_HEADS, XA_HD)
    return k, v


def cross_attend(h, k, v, w_q, w_o):
    b, L, _ = h.shape
    q = (h @ w_q).reshape(b, L, XA_HEADS, XA_HD)
    s = jnp.einsum('blhd,bmhd->bhlm', q, k).astype(jnp.float32) * (XA_HD ** -0.5)
    p = jax.nn.softmax(s, axis=-1).astype(v.dtype)
    o = jnp.einsum('bhlm,bmhd->blhd', p, v).reshape(b, L, D_MODEL)
    return o @ w_o


def setup_inputs(seed: int = 0) -> dict:
    key = jax.random.key(seed)
    ks = jax.random.split(key, 26)
    n_even = (DEPTH + 1) // 2
    n_odd = DEPTH // 2

    def nrm(k, shape, scale=1.0):
        return jax.random.normal(k, shape, jnp.float32) * scale

    return {
        "x_prompt": nrm(ks[0], (BATCH, SEQ, D_MODEL)),
        "x_sample": nrm(ks[1], (DEC_BATCH, DEC_SEQ, D_MODEL)),
        "mem_prompt": nrm(ks[2], (BATCH, N_MEM, D_MODEL)),
        "state_pool": nrm(ks[3], (n_even, DEC_BATCH, POOL_CTX, W_A)),
        "state_conv": nrm(ks[4], (n_odd, DEC_BATCH, CONV_CTX, W_C)),
        "cache_mem_k": nrm(ks[5], (DEPTH, DEC_BATCH, N_MEM, XA_HEADS, XA_HD)),
        "cache_mem_v": nrm(ks[6], (DEPTH, DEC_BATCH, N_MEM, XA_HEADS, XA_HD)),
        "norm_mix_g": 1.0 + nrm(ks[7], (DEPTH, D_MODEL), 0.05),
        "norm_xattn_g": 1.0 + nrm(ks[8], (DEPTH, D_MODEL), 0.05),
        "norm_mem_g": 1.0 + nrm(ks[9], (DEPTH, D_MODEL), 0.05),
        "w_in_ab": nrm(ks[10], (n_even, D_MODEL, 2 * W_A + 3 * W_B), D_MODEL ** -0.5),
        "pool_maps": nrm(ks[11], (n_even, POOL_GROUPS, POOL_GW, POOL_GW), POOL_GW ** -0.5),
        "pool_scale": 0.5 + nrm(ks[12], (n_even, W_A), 0.05),
        "sgu_w": nrm(ks[13], (n_even, SGU_GROUPS, CHUNK, CHUNK), CHUNK ** -0.5),
        "sgu_b": 1.0 + nrm(ks[14], (n_even, SGU_GROUPS, CHUNK), 0.1),
        "sgu_g": 1.0 + nrm(ks[15], (n_even, W_B), 0.05),
        "w_out_ab": nrm(ks[16], (n_even, W_A + W_B, D_MODEL), (W_A + W_B) ** -0.5),
        "w_in_c": nrm(ks[17], (n_odd, D_MODEL, 4 * W_C), D_MODEL ** -0.5),
        "conv_w": nrm(ks[18], (n_odd, CONV_WIDTH, W_C), CONV_WIDTH ** -0.5),
        "w_out_c": nrm(ks[19], (n_odd, W_C, D_MODEL), W_C ** -0.5),
        "w_q": nrm(ks[20], (DEPTH, D_MODEL, D_MODEL), D_MODEL ** -0.5),
        "w_k": nrm(ks[21], (DEPTH, D_MODEL, D_MODEL), D_MODEL ** -0.5),
        "w_v": nrm(ks[22], (DEPTH, D_MODEL, D_MODEL), D_MODEL ** -0.5),
        "w_o": nrm(ks[23], (DEPTH, D_MODEL, D_MODEL), D_MODEL ** -0.5),
        "norm_final_g": 1.0 + nrm(ks[24], (D_MODEL,), 0.05),
    }


def reference(x_prompt, x_sample, mem_prompt, state_pool, state_conv, cache_mem_k, cache_mem_v,
              norm_mix_g, norm_xattn_g, norm_mem_g, w_in_ab, pool_maps, pool_scale, sgu_w, sgu_b,
              sgu_g, w_out_ab, w_in_c, conv_w, w_out_c, w_q, w_k, w_v, w_o, norm_final_g):
    bp = x_prompt.shape[0]
    xp, xs = x_prompt, x_sample
    pool_p, pool_s, conv_p, conv_s, vrows_s, mem_k_p, mem_v_p = [], [], [], [], [], [], []
    for i in range(DEPTH):
        j = i // 2
        hp = rms_norm(xp, norm_mix_g[i])
        hs = rms_norm(xs, norm_mix_g[i])
        if i % 2 == 0:
            prm = (w_in_ab[j], pool_maps[j], pool_scale[j], sgu_w[j], sgu_b[j], sgu_g[j], w_out_ab[j])
            zero_buf = jnp.zeros((bp, POOL_CTX, W_A), xp.dtype)
            op, buf_p, _ = pool_sgu_mixer(hp, zero_buf, 0, *prm)
            os_, buf_s, v_s = pool_sgu_mixer(hs, state_pool[j], PAST_LEN, *prm)
            pool_p.append(buf_p)
            pool_s.append(buf_s)
            vrows_s.append(v_s)
        else:
            zero_buf = jnp.zeros((bp, CONV_CTX, W_C), xp.dtype)
            op, buf_p = short_conv_mixer(hp, zero_buf, w_in_c[j], conv_w[j], w_out_c[j])
            os_, buf_s = short_conv_mixer(hs, state_conv[j], w_in_c[j], conv_w[j], w_out_c[j])
            conv_p.append(buf_p)
            conv_s.append(buf_s)
        xp = xp + op
        xs = xs + os_
        kp, vp = memory_kv(mem_prompt, norm_mem_g[i], w_k[i], w_v[i])
        mem_k_p.append(kp)
        mem_v_p.append(vp)
        xp = xp + cross_attend(rms_norm(xp, norm_xattn_g[i]), kp, vp, w_q[i], w_o[i])
        xs = xs + cross_attend(rms_norm(xs, norm_xattn_g[i]), cache_mem_k[i], cache_mem_v[i], w_q[i], w_o[i])
    y_prompt = rms_norm(xp, norm_final_g)
    y_sample = rms_norm(xs, norm_final_g)
    return (y_prompt, y_sample, jnp.stack(pool_p), jnp.stack(pool_s), jnp.stack(conv_p), jnp.stack(conv_s),
            jnp.stack(vrows_s), jnp.stack(mem_k_p), jnp.stack(mem_v_p))
```

```python
from contextlib import ExitStack
import numpy as np
import concourse.bass as bass
import concourse.mybir as mybir
from concourse.bass_utils import run_bass_kernel_spmd

F32 = mybir.dt.float32
BF16 = mybir.dt.bfloat16
I32 = mybir.dt.int32
AF = mybir.ActivationFunctionType
ALU = mybir.AluOpType
AX = mybir.AxisListType

ENGS = ("pe", "act", "dve", "pool", "sp")
NCORES = 8
D = 1024
SEQ = 2048
NS = 16
NT = SEQ + NS
DEPTH = 4
EPS = 1e-6
BLK = [(0, 512), (512, 512), (1024, 512), (1536, 512), (2048, NS)]
POOL_W = (2, 4, 8, 16)
DBG = {"layers": DEPTH, "mix": True, "att": True}


class Prog:
    def __init__(self, nc, n_dma_sems=40):
        self.nc = nc
        self.ops = {e: [] for e in ENGS}
        self.cnt = {e: 0 for e in ENGS}
        self.waited = {e: {} for e in ENGS}
        self.last_w = {}
        self.readers = {}
        self.n_dma_sems = n_dma_sems
        self.dma_rr = {e: 0 for e in ENGS}
        self.dma_val = [0] * n_dma_sems
        self.dma_last_tok = [None] * n_dma_sems

    def _need(self, eng, toks):
        best = {}
        for t in toks:
            if t is None:
                continue
            k, v, _ = t
            if v > best.get(k, 0):
                best[k] = v
        out = []
        for k, v in best.items():
            if self.waited[eng].get(k, 0) >= v:
                continue
            self.waited[eng][k] = v
            out.append((k, v))
        return out

    @staticmethod
    def _flat(keys):
        out = []
        for k in keys:
            if isinstance(k, list):
                out.extend(Prog._flat(k))
            else:
                out.append(k)
        return out

    @staticmethod
    def _excl(reads, writes):
        reads = Prog._flat(reads); writes = Prog._flat(writes)
        ps = [k for k in reads if isinstance(k, tuple) and k[0] == "PS"]
        if ps:
            reads = [k for k in reads if not (isinstance(k, tuple) and k[0] == "PS")]
            writes = writes + [k for k in ps if k not in writes]
        return reads, writes

    def _deps(self, eng, reads, writes, extra):
        reads, writes = self._excl(reads, writes)
        toks = list(extra)
        for r in reads:
            toks.append(self.last_w.get(r))
        for w in writes:
            toks.append(self.last_w.get(w))
            for t in self.readers.get(w, ()):
                toks.append(t)
        return self._need(eng, toks)

    def _reg(self, tok, reads, writes):
        reads, writes = self._excl(reads, writes)
        for r in reads:
            self.readers.setdefault(r, []).append(tok)
        for w in writes:
            self.last_w[w] = tok
            self.readers[w] = []

    mute = False

    def group(self, eng, fns, reads=(), writes=(), extra=()):
        if self.mute:
            return None
        waits = self._deps(eng, reads, writes, extra)
        self.cnt[eng] += 1
        tok = (eng, self.cnt[eng], eng)
        n = len(fns)
        for i, fn in enumerate(fns):
            self.ops[eng].append((waits if i == 0 else (), fn, ("eng", eng) if i == n - 1 else None))
        self._reg(tok, reads, writes)
        return tok

    def op(self, eng, fn, reads=(), writes=(), extra=()):
        return self.group(eng, [fn], reads, writes, extra)

    def dma(self, q, fn, reads=(), writes=(), extra=()):
        if self.mute:
            return None
        lo, hi = (0, 24) if q != "pool" else (24, self.n_dma_sems)
        i = lo + self.dma_rr[q] % (hi - lo)
        self.dma_rr[q] += 1
        reads = self._flat(reads); writes = self._flat(writes)
        toks = list(extra)
        toks.append(self.dma_last_tok[i])
        for r in reads:
            toks.append(self.last_w.get(r))
        for w in writes:
            toks.append(self.last_w.get(w))
            toks.extend(self.readers.get(w, ()))
        waits = self._need(q, toks)
        self.dma_val[i] += 16
        tok = (("dma", i), self.dma_val[i], None)
        self.dma_last_tok[i] = tok
        self.ops[q].append((waits, fn, ("dma", i)))
        self._reg(tok, reads, writes)
        return tok

    def wait(self, eng, toks):
        waits = self._need(eng, toks)
        if waits:
            self.ops[eng].append((waits, None, None))

    def replay(self, stack):
        nc = self.nc
        semh = {}
        for e in ENGS:
            semh[e] = stack.enter_context(nc.semaphore("s_" + e))
        for i in range(self.n_dma_sems):
            semh[("dma", i)] = stack.enter_context(nc.semaphore("s_dma%d" % i))
        block = stack.enter_context(nc.Block())
        hmap = {"pe": block.tensor, "act": block.scalar, "dve": block.vector,
                "pool": block.gpsimd, "sp": block.sync}

        def mk(e):
            def body(eng):
                for waits, fn, sig in self.ops[e]:
                    for k, v in waits:
                        eng.wait_ge(semh[k], v)
                    if fn is None:
                        continue
                    ins = fn(eng)
                    if sig is not None:
                        if sig[0] == "eng":
                            ins.then_inc(semh[sig[1]], 1)
                        else:
                            ins.then_inc(semh[sig], 16)
            return body

        for e in ENGS:
            hmap[e](mk(e))


def I(method, *args, **kw):
    return lambda e: getattr(e, method)(*args, **kw)


def build_program():
    nc = bass.Bass("TRN2", target_bir_lowering=False)

    def din(name, shape):
        return nc.dram_tensor(name, list(shape), F32, kind="ExternalInput").ap()

    def dout(name, shape):
        return nc.dram_tensor(name, list(shape), F32, kind="ExternalOutput").ap()

    xp_d = din("xp", [SEQ, D]); xs_d = din("xs", [NS, D]); mem_d = din("mem", [256, D])
    spool_d = din("spool", [2, NS, 15, D]); sconv_d = din("sconv", [2, NS, 2, 2 * D])
    ck_d = din("ck", [DEPTH, NS, 256, D]); cv_d = din("cv", [DEPTH, NS, 256, D])
    g_mix_d = din("norm_mix_g", [DEPTH, D]); g_xa_d = din("norm_xattn_g", [DEPTH, D])
    g_mem_d = din("norm_mem_g", [DEPTH, D])
    w_in_ab_d = din("w_in_ab", [2, D, 5 * D]); pmaps_d = din("pool_maps", [2, 4, 256, 256])
    pscale_d = din("pool_scale", [2, D]); sgu_w_d = din("sgu_w", [2, 4, 128, 128])
    sgu_b_d = din("sgu_b", [2, 4, 128]); sgu_g_d = din("sgu_g", [2, D])
    w_out_ab_d = din("w_out_ab", [2, 2 * D, D]); w_in_c_d = din("w_in_c", [2, D, 8 * D])
    conv_w_d = din("conv_w", [2, 3, 2 * D]); w_out_c_d = din("w_out_c", [2, 2 * D, D])
    wq_d = din("w_q", [DEPTH, D, D]); wk_d = din("w_k", [DEPTH, D, D])
    wv_d = din("w_v", [DEPTH, D, D]); wo_d = din("w_o", [DEPTH, D, D])
    g_fin_d = din("norm_final_g", [D])

    yp_d = dout("yp", [SEQ, D]); ys_d = dout("ys", [NS, D])
    npp_d = dout("npp", [2, 15, D]); nps_d = dout("nps", [2, NS, 15, D])
    ncp_d = dout("ncp", [2, 2, 2 * D]); ncs_d = dout("ncs", [2, NS, 2, 2 * D])
    nsv_d = dout("nsv", [2, NS, D]); nmk_d = dout("nmk", [DEPTH, 256, D]); nmv_d = dout("nmv", [DEPTH, 256, D])

    P = Prog(nc)
    out_toks = []
    with ExitStack() as st:
        def sb(name, shape, dt):
            return st.enter_context(nc.sbuf_tensor(name, list(shape), dt))

        XP = sb("XP", [128, 8, NT], F32)
        H = sb("H", [128, 8, NT], BF16)
        YA = sb("YA", [128, 8, NT], BF16)
        NW = 4
        WB = sb("WB", [128, NW, 2048], BF16)
        TT = sb("TT", [128, 3, 2080], F32)
        TB = sb("TB", [128, 2, NT], BF16)
        KT = sb("KT", [128, 8, 256], BF16)
        VM = sb("VM", [128, 2, 1024], BF16)
        GV = sb("GV", [128, 216], F32)
        OST = sb("OST", [32, 1024], F32)
        RS = sb("RS", [128, 512], F32)
        ident = sb("ident", [128, 128], F32)
        onesf = sb("onesf", [128, 128], F32)
        onesb = sb("onesb", [128, 128], BF16)
        SEL = sb("SEL", [16, 16, 128], BF16)
        INVT = sb("INVT", [128, 4, 16], F32)
        IOT = sb("IOT", [128, 16], I32)
        WT = sb("WT", [128, 4, 128], BF16)
        WSC = sb("WSC", [16, 4], F32)
        DG = sb("DG", [16, 4, 16], BF16)
        XAS = sb("XAS", [128, 8, NS], F32)
        XAL = sb("XAL", [128, 8, 32], F32)
        CSUM = sb("CSUM", [128, 8, NS], F32)
        SM1 = sb("SM1", [128, 64], F32)
        SM2 = sb("SM2", [128, 64], F32)
        SCX = sb("SCX", [128, 16, 32], F32)
        CXL = sb("CXL", [128, 2, 16], F32)
        CXS = sb("CXS", [128, 16, NS], F32)
        SC = sb("SC", [128, 16, 2, 4], F32)
        ES = sb("ES", [128, 16, 2, 4], BF16)
        DS = sb("DS", [128, 16, 4], F32)
        VST = sb("VST", [128, 2], F32)
        PSA = st.enter_context(nc.psum_tensor("PSA", [128, 8, 512], F32))

        QS = OST[0:16, 0:512].bitcast(BF16)
        T0, T1, T2 = TT[:, 0, :], TT[:, 1, :], TT[:, 2, :]
        GVT = TT[:, 2, 0:1024]
        BS8 = TT[:, 2, 1024:2048].rearrange("p (c t) -> p c t", c=8)
        TBf = TB[:].rearrange("p a b -> p (a b)")
        TK = [["T0a", "T0b"], ["T1a", "T1b"], ["T2a", "T2b"]]
        TBALL = [("TB", r, b_) for r in range(2) for b_ in range(5)]

        def bank(b):
            return PSA[:, b, :]

        def psk(*bs):
            return [("PS", b) for b in bs]

        P.op("pool", I("memset", onesf[:], 1.0), writes=["onesf"])
        P.op("pool", I("memset", onesb[:], 1.0), writes=["onesb"])
        P.op("pool", I("affine_select", out=ident[:], in_=onesf[:], pattern=[[-1, 128]],
                                               compare_op=ALU.is_equal, fill=0.0, base=0,
                                               channel_multiplier=1),
             reads=["onesf"], writes=["ident"])
        P.op("pool", I("memset", SEL[:], 1.0), writes=["SEL"])
        P.op("pool", I("affine_select", out=SEL[:], in_=SEL[:], pattern=[[-1, 16], [0, 128]],
                                               compare_op=ALU.is_equal, fill=0.0, base=0,
                                               channel_multiplier=1),
             reads=["SEL"], writes=["SEL"])
        P.op("pool", I("iota", IOT[:], pattern=[[1, 16]], base=1, channel_multiplier=0),
             writes=["IOT"])
        P.op("dve", I("tensor_copy", out=SM1[:, 0:16], in_=IOT[:]), reads=["IOT"], writes=["SM1"])
        for g, w in enumerate(POOL_W):
            P.op("dve", I("tensor_scalar", out=INVT[:, g, :], in0=SM1[:, 0:16],
                                                           scalar1=float(w), scalar2=None, op0=ALU.min),
                 reads=["SM1"], writes=["INVT"])
        P.op("dve", I("reciprocal", out=INVT[:], in_=INVT[:]), reads=["INVT"], writes=["INVT"])

        ga = T0[0:120, 0:128]
        gb_ = T0[0:96, 128:256]
        for k, (src, r0) in enumerate([(g_mix_d, 0), (g_xa_d, 32), (g_mem_d, 64)]):
            P.dma("sp", I("dma_start",
                out=T0[r0:r0 + 32, 0:128], in_=src.rearrange("i (kc p) -> (i kc) p", p=128)), writes=[TK[0]])
        P.dma("sp", I("dma_start", out=T0[96:112, 0:128],
                                          in_=pscale_d.rearrange("j (kc p) -> (j kc) p", p=128)), writes=[TK[0]])
        P.dma("sp", I("dma_start", out=T0[112:120, 0:128],
                                          in_=g_fin_d.rearrange("(kc p) -> kc p", p=128)), writes=[TK[0]])
        P.dma("sp", I("dma_start", out=T0[0:96, 128:256],
                                          in_=conv_w_d.rearrange("j k (fc p) -> (j k fc) p", p=128)), writes=[TK[0]])
        P.group("pe", [I("transpose", out=PSA[:, 4, 0:120], in_=ga, identity=ident[0:120, 0:120]),
                       I("transpose", out=PSA[:, 4, 128:224], in_=gb_, identity=ident[0:96, 0:96])],
                reads=[TK[0], "ident"], writes=psk(4))
        P.op("dve", I("tensor_copy", out=GV[:, 0:120], in_=PSA[:, 4, 0:120]), reads=psk(4), writes=["GV"])
        P.op("dve", I("tensor_copy", out=GV[:, 120:216], in_=PSA[:, 4, 128:224]), reads=psk(4), writes=["GV"])

        def gcol(c):
            return GV[:, c:c + 1]

        wstate = {"slot": 0}

        def wload(src3):
            s = wstate["slot"]
            wstate["slot"] = (s + 1) % NW
            a, b = src3.shape[1], src3.shape[2]
            dst = WB[:, s, 0:a * b].rearrange("p (a b) -> p a b", a=a)
            P.dma("pool", I("dma_start", out=dst, in_=src3), writes=[("W", s)])
            return dst, ("W", s)

        def wcols(wd, r0, nk, c0, ncol):
            return wd[r0:r0 + nk * 128, c0:c0 + ncol].rearrange("(kc p) c -> p kc c", p=128)

        acc = {"b": 0}

        def next_acc():
            b = acc["b"]
            acc["b"] = (b + 1) % 8
            return b

        def proj(wv, wkey, col0, nk, src, srckey, evac, blks=range(5)):
            for bi in blks:
                t0, n = BLK[bi]
                b = next_acc()
                fns = []
                for kc in range(nk):
                    fns.append(I("matmul",
                        PSA[:, b, 0:n], lhsT=wv[:, kc, col0:col0 + 128], rhs=src[:, kc, t0:t0 + n],
                        start=(kc == 0), stop=(kc == nk - 1)))
                P.group("pe", fns, reads=[wkey] + [(srckey, kc, bi) for kc in range(nk)], writes=psk(b))
                evac(bi, b, n)

        SQK = [[("TB", 0, b_) for b_ in range(4)], [("TB", 0, 4)] + [("TB", 1, b_) for b_ in range(4)]]
        RSK = [("RS", 0), ("RS", 1)]
        nrm = {"k": 0}

        def rms_norm_fm(gbase, dst, dstkey, fp32_out=False):
            subs = []
            for bi in range(5):
                t0b, nb = BLK[bi]
                for t0 in range(t0b, t0b + nb, 256):
                    subs.append((bi, t0, min(256, t0b + nb - t0), nrm["k"] % 2))
                    nrm["k"] += 1

            def st_a(bi, t0, n, p):
                sq = TBf[:, p * 2048:p * 2048 + 8 * n].rearrange("p (k n) -> p k n", k=8)
                P.op("act", I("activation", out=sq, in_=XP[:, :, t0:t0 + n], func=AF.Square),
                     reads=[("XP", kc, bi) for kc in range(8)], writes=[SQK[p]])
                P.group("pe", [I("matmul", PSA[:, 4 + p, 0:n], lhsT=onesb[:], rhs=sq[:, kc, :],
                                 start=(kc == 0), stop=(kc == 7)) for kc in range(8)],
                        reads=[SQK[p], "onesb"], writes=psk(4 + p))

            def st_b(bi, t0, n, p):
                rs = RS[:, p * 256:p * 256 + n]
                P.op("act", I("activation", out=rs, in_=PSA[:, 4 + p, 0:n], func=AF.Ln, bias=EPSC[:, 0:1], scale=1.0 / D),
                     reads=psk(4 + p) + ["EPSC"], writes=[RSK[p]])
                P.op("act", I("activation", out=rs, in_=rs, func=AF.Exp, scale=-0.5), reads=[RSK[p]], writes=[RSK[p]])
                for kc in range(8):
                    P.op("dve", I("scalar_tensor_tensor", out=dst(kc, t0, n), in0=XP[:, kc, t0:t0 + n],
                                  scalar=gcol(gbase + kc), in1=rs, op0=ALU.mult, op1=ALU.mult),
                         reads=[("XP", kc, bi), RSK[p], "GV"], writes=[(dstkey, kc, bi)])

            st_a(*subs[0])
            for k in range(1, len(subs)):
                st_a(*subs[k])
                st_b(*subs[k - 1])
            st_b(*subs[-1])

        EPSC = sb("EPSC", [128, 1], F32)
        P.op("pool", I("memset", EPSC[:], EPS), writes=["EPSC"])

        def resid_add(fc, srckey_unused=None):
            def evac(bi, b, n):
                t0 = BLK[bi][0]
                P.op("dve", I("tensor_tensor", out=XP[:, fc, t0:t0 + n], in0=PSA[:, b, 0:n],
                                                      in1=XP[:, fc, t0:t0 + n], op=ALU.add),
                     reads=psk(b) + [("XP", fc, bi)], writes=[("XP", fc, bi)])
            return evac

        def out_proj(wd, r0, src, srckey, hooks=None):
            for pr in range(4):
                if hooks and pr in hooks:
                    hooks[pr]()
                wv, wkey = wload(wcols(wd, r0, 8, pr * 256, 256))
                for jj in range(2):
                    fc = pr * 2 + jj
                    proj(wv, wkey, jj * 128, 8, src, srckey, resid_add(fc))

        for q8 in range(8):
            xst = TT[:, q8 % 2, 0:2048].rearrange("p (a b) -> p a b", a=2)
            xk = TK[q8 % 2]
            P.dma("sp", I("dma_start", out=xst, in_=xp_d[q8 * 256:(q8 + 1) * 256, :].rearrange("(t p) f -> p t f", p=128)),
                  writes=[xk])
            for kc in range(8):
                b = next_acc()
                P.group("pe", [I("transpose", out=PSA[:, b, t * 128:(t + 1) * 128], in_=xst[:, t, kc * 128:(kc + 1) * 128],
                                 identity=ident[:]) for t in range(2)], reads=[xk, "ident"], writes=psk(b))
                if kc % 2:
                    P.op("act", I("activation", out=XP[:, kc, q8 * 256:(q8 + 1) * 256], in_=PSA[:, b, 0:256], func=AF.Copy),
                         reads=psk(b), writes=[("XP", kc, q8 // 2)])
                else:
                    P.op("dve", I("tensor_copy", out=XP[:, kc, q8 * 256:(q8 + 1) * 256], in_=PSA[:, b, 0:256]),
                         reads=psk(b), writes=[("XP", kc, q8 // 2)])
        P.dma("sp", I("dma_start", out=OST[0:16, :], in_=xs_d), writes=["OST"])
        b = next_acc()
        P.group("pe", [I("transpose", out=PSA[:, b, kc * 16:(kc + 1) * 16],
                                                         in_=OST[0:16, kc * 128:(kc + 1) * 128],
                                                         identity=ident[0:16, 0:16]) for kc in range(8)],
                reads=["OST", "ident"], writes=psk(b))
        P.op("dve", I("tensor_copy", out=XP[:, :, 2048:2064],
                                                 in_=PSA[:, b, 0:128].rearrange("p (k s) -> p k s", k=8)),
             reads=psk(b), writes=[("XP", kc, 4) for kc in range(8)])

        def even_layer(i):
            j = i // 2
            wab = w_in_ab_d[j]
            P.dma("sp", I("dma_start", out=GVT, in_=sgu_g_d[j].partition_broadcast(128)), writes=[TK[2]])
            for rep in range(2):
                P.dma("sp", I("dma_start",
                    out=BS8.rearrange("p (g two) t -> p g two t", two=2)[:, :, rep, :],
                    in_=sgu_b_d[j].rearrange("g t -> (g t)").partition_broadcast(128).rearrange("p (g t) -> p g t", g=4)),
                    writes=[TK[2]])
            P.dma("sp", I("dma_start", out=WSC[:], in_=sgu_w_d[j][:, 0, 0].partition_broadcast(16), allow_slow_non_contiguous=True),
                  writes=["WSC"])
            wst = T1[:, 0:512].rearrange("p (g s) -> p g s", g=4)
            P.dma("sp", I("dma_start", out=wst, in_=sgu_w_d[j].rearrange("g t s -> t g s")), writes=[TK[1]])
            P.group("pe", [I("transpose", out=PSA[:, 5, g * 128:(g + 1) * 128], in_=wst[:, g, :],
                                                      identity=ident[:]) for g in range(4)],
                    reads=[TK[1], "ident"], writes=psk(5))
            P.op("dve", I("tensor_copy", out=T1[:, 512:1024], in_=PSA[:, 5, :]), reads=psk(5), writes=[TK[1]])
            P.op("pool", I("affine_select", out=WT[:], in_=T1[:, 512:1024].rearrange("p (g t) -> p g t", g=4),
                                                   pattern=[[0, 4], [1, 128]], compare_op=ALU.is_ge, fill=0.0,
                                                   base=0, channel_multiplier=-1),
                 reads=[TK[1]], writes=["WT"])
            for g in range(4):
                P.op("dve", I("tensor_scalar", out=DG[:, g, :], in0=ident[0:16, 0:16],
                                                           scalar1=WSC[:, g:g + 1], scalar2=None, op0=ALU.mult),
                     reads=["WSC", "ident"], writes=["DG"])
            spt = T0[0:120, 0:2048].rearrange("p (a c) -> p a c", a=2)
            P.dma("sp", I("dma_start", out=spt, in_=spool_d[j].rearrange("(a s) r c -> (s r) a c", a=2)),
                  writes=[TK[0]])
            out_toks.append(P.dma("sp", I("dma_start", out=nps_d[j][:, 0:14, :], in_=spool_d[j][:, 1:15, :])))
            for a in range(2):
                for cc in range(8):
                    b = next_acc()
                    P.op("pe", I("transpose", out=PSA[:, b, 0:120],
                                                                       in_=spt[:, a, cc * 128:(cc + 1) * 128],
                                                                       identity=ident[0:120, 0:120]),
                         reads=[TK[0], "ident"], writes=psk(b))
                    w = POOL_W[cc // 2]
                    P.op("dve", I("tensor_reduce",
                        out=CSUM[:, cc, a * 8:(a + 1) * 8],
                        in_=PSA[:, b, 0:120].rearrange("p (s r) -> p s r", s=8)[:, :, 16 - w:15],
                        axis=AX.X, op=ALU.add),
                        reads=psk(b), writes=["CSUM"])

            rms_norm_fm(i * 8, lambda kc, t0, n: H[:, kc, t0:t0 + n], "H")

            wvs = [wload(wcols(wab, 0, 8, 3 * D + q * 256, 256)) for q in range(4)]
            jk = TT[:, 0, :].bitcast(BF16)[:, 0:1024].rearrange("p (a b) -> p a b", a=2)

            def e1_cfg(ti):
                t0, m = (ti * 128, 128) if ti < 16 else (2048, NS)
                bi = ti // 4 if ti < 16 else 4
                par = ti % 2
                vb = (4, 5) if par == 0 else (0, 1)
                mb = (6, 7) if par == 0 else (2, 3)
                vnk = [("TB", par, b_) for b_ in range(5)]
                return t0, m, bi, par, vb, mb, vnk, VST[0:m, par:par + 1], ("VST", par)

            def e1_a(ti):
                t0, m, bi, par, vb, mb, vnk, vst, vstk = e1_cfg(ti)
                fns = []
                for q in range(4):
                    for kc in range(8):
                        fns.append(I("matmul", PSA[0:m, vb[q // 2], (q % 2) * 256:(q % 2) * 256 + 256],
                                     lhsT=H[:, kc, t0:t0 + m], rhs=wvs[q][0][:, kc, :], start=(kc == 0), stop=(kc == 7)))
                P.group("pe", fns, reads=[k for _, k in wvs] + [("H", kc, bi) for kc in range(8)], writes=psk(*vb))
                vps = PSA[0:m, vb[0]:vb[0] + 2, :]
                vn = TB[0:m, par, 0:1024]
                P.op("act", I("activation", out=jk[0:m], in_=vps, func=AF.Square, accum_out=vst),
                     reads=psk(*vb), writes=[TK[0], vstk])
                P.op("act", I("activation", out=vst, in_=vst, func=AF.Sqrt, bias=EPSC[0:m, 0:1], scale=1.0 / D),
                     reads=[vstk, "EPSC"], writes=[vstk])
                P.op("dve", I("reciprocal", out=vst, in_=vst), reads=[vstk], writes=[vstk])
                P.op("dve", I("scalar_tensor_tensor", out=vn.rearrange("p (a b) -> p a b", a=2), in0=vps, scalar=vst,
                              in1=GVT[0:m, :].rearrange("p (a b) -> p a b", a=2), op0=ALU.mult, op1=ALU.mult),
                     reads=psk(*vb) + [vstk, TK[2]], writes=[vnk])
                if ti == 16:
                    P.op("dve", I("scalar_tensor_tensor", out=OST[0:16, :].rearrange("p (a b) -> p a b", a=2), in0=vps,
                                  scalar=vst, in1=GVT[0:16, :].rearrange("p (a b) -> p a b", a=2),
                                  op0=ALU.mult, op1=ALU.mult),
                         reads=psk(*vb) + [vstk, TK[2]], writes=["OST"])
                    out_toks.append(P.dma("sp", I("dma_start", out=nsv_d[j], in_=OST[0:16, :]), reads=["OST"]))

            def e1_b(ti):
                t0, m, bi, par, vb, mb, vnk, vst, vstk = e1_cfg(ti)
                vn = TB[0:m, par, 0:1024]
                if ti == 16:
                    fns = [I("matmul", PSA[:, mb[0], cc * 16:(cc + 1) * 16], lhsT=vn[:, cc * 128:(cc + 1) * 128],
                             rhs=DG[:, cc // 2, :], start=True, stop=True) for cc in range(8)]
                    P.group("pe", fns, reads=[vnk, "DG"], writes=psk(mb[0]))
                    for cc in range(8):
                        P.op("dve", I("tensor_scalar", out=YA[:, cc, 2048:2064], in0=PSA[:, mb[0], cc * 16:(cc + 1) * 16],
                                      scalar1=BS8[:, cc, 0:1], scalar2=None, op0=ALU.add),
                             reads=psk(mb[0]) + [TK[2]], writes=[("YA", cc, 4)])
                else:
                    fns = [I("matmul", PSA[:, mb[cc // 4], (cc % 4) * 128:(cc % 4) * 128 + 128],
                             lhsT=vn[:, cc * 128:(cc + 1) * 128], rhs=WT[:, cc // 2, :], start=True, stop=True)
                           for cc in range(8)]
                    P.group("pe", fns, reads=[vnk, "WT"], writes=psk(*mb))
                    P.op("dve", I("tensor_tensor", out=YA[:, :, t0:t0 + 128],
                                  in0=PSA[:, mb[0]:mb[0] + 2, :].rearrange("p a (c t) -> p (a c) t", c=4),
                                  in1=BS8, op=ALU.add),
                         reads=psk(*mb) + [TK[2]], writes=[("YA", cc, bi) for cc in range(8)])

            e1_a(0)
            for ti in range(1, 17):
                e1_a(ti)
                e1_b(ti - 1)
            e1_b(16)

            for pr in range(4):
                wg, wgk = wload(wcols(wab, 0, 8, 4 * D + pr * 256, 256))
                wu, wuk = wload(wcols(wab, 0, 8, 2 * D + pr * 256, 256))
                for jj in range(2):
                    def ev_gb(bi, b, n, jj=jj):
                        t0 = BLK[bi][0]
                        P.op("act", I("activation", out=TB[:, jj, t0:t0 + n], in_=PSA[:, b, 0:n], func=AF.Silu),
                             reads=psk(b), writes=[("TB", jj, bi)])
                    proj(wg, wgk, jj * 128, 8, H, "H", ev_gb)
                for jj in range(2):
                    cc = pr * 2 + jj

                    def ev_u(bi, b, n, jj=jj, cc=cc):
                        t0 = BLK[bi][0]
                        P.op("dve", I("tensor_tensor", out=YA[:, cc, t0:t0 + n], in0=PSA[:, b, 0:n],
                                                              in1=YA[:, cc, t0:t0 + n], op=ALU.mult),
                             reads=psk(b) + [("YA", cc, bi)], writes=[("YA", cc, bi)])
                        P.op("dve", I("tensor_tensor", out=YA[:, cc, t0:t0 + n], in0=YA[:, cc, t0:t0 + n],
                                                              in1=TB[:, jj, t0:t0 + n], op=ALU.mult),
                             reads=[("YA", cc, bi), ("TB", jj, bi)], writes=[("YA", cc, bi)])
                    proj(wu, wuk, jj * 128, 8, H, "H", ev_u)
            out_proj(w_out_ab_d[j], D, YA, "YA")

            for k in range(3):
                P.op("dve", I("memset", TT[:, k, 0:16], 0.0), writes=[TK[k]])
            for g in range(4):
                w = POOL_W[g]
                wx, wxk = wload(wcols(wab, 0, 8, g * 256, 256))
                wga, wgak = wload(wcols(wab, 0, 8, D + g * 256, 256))
                pm, pmk = wload(pmaps_d[j, g].rearrange("(kc p) d -> p kc d", p=128))
                for jj in range(2):
                    cc = g * 2 + jj

                    def ev_xa(bi, b, n, cc=cc):
                        t0 = BLK[bi][0]
                        if bi == 4:
                            P.op("act", I("activation", out=XAS[:, cc, :], in_=PSA[:, b, 0:n], func=AF.Copy),
                                 reads=psk(b), writes=["XAS"])
                        else:
                            P.op("act", I("activation", out=T0[:, 16 + t0:16 + t0 + n], in_=PSA[:, b, 0:n],
                                                               func=AF.Copy),
                                 reads=psk(b), writes=[TK[0]])
                            if bi == 3:
                                P.op("act", I("activation", out=XAL[:, cc, :], in_=PSA[:, b, 480:512],
                                                                   func=AF.Copy),
                                     reads=psk(b), writes=["XAL"])
                    proj(wx, wxk, jj * 128, 8, H, "H", ev_xa)
                    bufs = [T0, T1, T2]
                    keys = [TK[0], TK[1], TK[2]]
                    cur = 0
                    sh = 1
                    nxt = 1
                    while sh < w:
                        P.op("dve", I("tensor_tensor",
                            out=bufs[nxt][:, 16:2064], in0=bufs[cur][:, 16:2064], in1=bufs[cur][:, 16 - sh:2064 - sh],
                            op=ALU.add), reads=[keys[cur]], writes=[keys[nxt]])
                        cur = nxt
                        nxt = 2 if cur == 1 else 1
                        sh *= 2
                    S, Sk = bufs[cur], keys[cur]
                    P.op("dve", I("scalar_tensor_tensor",
                        out=TB[:, jj, 0:2048], in0=S[:, 16:2064], scalar=1.0 / w, in1=T0[:, 16:2064],
                        op0=ALU.mult, op1=ALU.subtract), reads=[Sk, TK[0]], writes=[("TB", jj, bi) for bi in range(4)])
                    P.op("dve", I("tensor_tensor", out=SM1[:, 0:16], in0=S[:, 16:32], in1=INVT[:, g, :],
                                                                  op=ALU.mult), reads=[Sk, "INVT"], writes=["SM1"])
                    P.op("dve", I("tensor_tensor", out=TB[:, jj, 0:16], in0=SM1[:, 0:16], in1=T0[:, 16:32],
                                                                 op=ALU.subtract),
                         reads=["SM1", TK[0]], writes=[("TB", jj, 0)])
                    P.op("dve", I("tensor_tensor", out=SM2[:, 0:16], in0=CSUM[:, cc, :], in1=XAS[:, cc, :],
                                                                 op=ALU.add), reads=["CSUM", "XAS"], writes=["SM2"])
                    P.op("dve", I("scalar_tensor_tensor",
                        out=TB[:, jj, 2048:2064], in0=SM2[:, 0:16], scalar=1.0 / w, in1=XAS[:, cc, :],
                        op0=ALU.mult, op1=ALU.subtract), reads=["SM2", "XAS"], writes=[("TB", jj, 4)])
                for jj in range(2):
                    cc = g * 2 + jj

                    def ev_ga(bi, b, n, cc=cc):
                        t0 = BLK[bi][0]
                        P.op("act", I("activation", out=YA[:, cc, t0:t0 + n], in_=PSA[:, b, 0:n], func=AF.Silu),
                             reads=psk(b), writes=[("YA", cc, bi)])
                    proj(wga, wgak, jj * 128, 8, H, "H", ev_ga)
                for jj in range(2):
                    cc = g * 2 + jj

                    def ev_ya(bi, b, n, cc=cc):
                        t0 = BLK[bi][0]
                        P.op("dve", I("scalar_tensor_tensor",
                            out=YA[:, cc, t0:t0 + n], in0=PSA[:, b, 0:n], scalar=gcol(96 + j * 8 + cc),
                            in1=YA[:, cc, t0:t0 + n], op0=ALU.mult, op1=ALU.mult),
                            reads=psk(b) + [("YA", cc, bi), "GV"], writes=[("YA", cc, bi)])
                    proj(pm, pmk, jj * 128, 2, TB, "TB", ev_ya)
            P.group("pe", [I("transpose", out=PSA[0:32, 4 + cc // 4, (cc % 4) * 128:(cc % 4) * 128 + 128],
                                                        in_=XAL[:, cc, :], identity=ident[:]) for cc in range(8)],
                    reads=["XAL", "ident"], writes=psk(4, 5))
            P.op("dve", I("tensor_copy", out=OST[:, :].rearrange("p (a b) -> p a b", a=2), in_=PSA[0:32, 4:6, :]),
                 reads=psk(4, 5), writes=["OST"])
            out_toks.append(P.dma("sp", I("dma_start", out=npp_d[j], in_=OST[17:32, :]), reads=["OST"]))
            P.group("pe", [I("transpose", out=PSA[0:16, 4 + cc // 4, (cc % 4) * 128:(cc % 4) * 128 + 128],
                                                        in_=XAS[:, cc, :], identity=ident[:]) for cc in range(8)],
                    reads=["XAS", "ident"], writes=psk(4, 5))
            P.op("dve", I("tensor_copy", out=OST[0:16, :].rearrange("p (a b) -> p a b", a=2), in_=PSA[0:16, 4:6, :]),
                 reads=psk(4, 5), writes=["OST"])
            out_toks.append(P.dma("sp", I("dma_start", out=nps_d[j][:, 14, :], in_=OST[0:16, :]), reads=["OST"]))
            out_proj(w_out_ab_d[j], 0, YA, "YA")

        def dummy_h():
            return None

        def odd_layer(i):
            j = i // 2
            wc = w_in_c_d[j]
            sct = T0[0:32, 0:2048]
            P.dma("sp", I("dma_start", out=sct, in_=sconv_d[j].rearrange("s r c -> (s r) c")), writes=[TK[0]])
            out_toks.append(P.dma("sp", I("dma_start", out=ncs_d[j][:, 0, :], in_=sconv_d[j][:, 1, :])))
            for q in range(4):
                P.group("pe", [I("transpose", out=PSA[:, 4, k * 32:(k + 1) * 32],
                                                              in_=sct[:, (q * 4 + k) * 128:(q * 4 + k + 1) * 128],
                                                              identity=ident[0:32, 0:32]) for k in range(4)],
                        reads=[TK[0], "ident"], writes=psk(4))
                P.op("dve", I("tensor_copy", out=SCX[:, q * 4:(q + 1) * 4, :],
                                                         in_=PSA[:, 4, 0:128].rearrange("p (k x) -> p k x", k=4)),
                     reads=psk(4), writes=["SCX"])
            rms_norm_fm(i * 8, lambda kc, t0, n: H[:, kc, t0:t0 + n], "H")
            P.op("dve", I("memset", T1[:, 0:16], 0.0), writes=[TK[1]])
            for half in range(2):
                for pr in range(4):
                    c0 = half * D + pr * 256
                    wcg, wcgk = wload(wcols(wc, 0, 8, 2 * D + c0, 256))
                    wxc, wxck = wload(wcols(wc, 0, 8, 4 * D + c0, 256))
                    wbg, wbgk = wload(wcols(wc, 0, 8, c0, 256))
                    wgg, wggk = wload(wcols(wc, 0, 8, 6 * D + c0, 256))
                    for jj in range(2):
                        fc = half * 8 + pr * 2 + jj
                        yc = fc % 8
                        cw = 120 + j * 48 + fc

                        def ev_cg(bi, b, n):
                            t0 = BLK[bi][0]
                            P.op("act", I("activation", out=T0[:, 16 + t0:16 + t0 + n], in_=PSA[:, b, 0:n],
                                                               func=AF.Copy), reads=psk(b), writes=[TK[0]])
                        proj(wcg, wcgk, jj * 128, 8, H, "H", ev_cg)

                        def ev_xc(bi, b, n, fc=fc):
                            t0 = BLK[bi][0]
                            P.op("dve", I("tensor_tensor", out=T1[:, 16 + t0:16 + t0 + n], in0=PSA[:, b, 0:n],
                                                                  in1=T0[:, 16 + t0:16 + t0 + n], op=ALU.mult),
                                 reads=psk(b) + [TK[0]], writes=[TK[1]])
                        proj(wxc, wxck, jj * 128, 8, H, "H", ev_xc)
                        P.op("act", I("activation", out=CXL[:, :, fc], in_=T1[:, 2062:2064], func=AF.Copy),
                             reads=[TK[1]], writes=["CXL"])
                        P.op("act", I("activation", out=CXS[:, fc, :], in_=T1[:, 2064:2080], func=AF.Copy),
                             reads=[TK[1]], writes=["CXS"])
                        P.op("act", I("activation", out=T2[:, 16:2064], in_=T1[:, 14:2062], func=AF.Copy,
                                                                  scale=gcol(cw)), reads=[TK[1], "GV"], writes=[TK[2]])
                        P.op("dve", I("scalar_tensor_tensor",
                            out=T2[:, 16:2064], in0=T1[:, 15:2063], scalar=gcol(cw + 16), in1=T2[:, 16:2064],
                            op0=ALU.mult, op1=ALU.add), reads=[TK[1], TK[2], "GV"], writes=[TK[2]])
                        P.op("dve", I("scalar_tensor_tensor",
                            out=T2[:, 16:2064], in0=T1[:, 16:2064], scalar=gcol(cw + 32), in1=T2[:, 16:2064],
                            op0=ALU.mult, op1=ALU.add), reads=[TK[1], TK[2], "GV"], writes=[TK[2]])
                        scx = SCX[:, fc, :].rearrange("p (s r) -> p s r", r=2)
                        P.op("dve", I("tensor_scalar",
                            out=T2[:, 2064:2080], in0=scx[:, :, 0], scalar1=gcol(cw), scalar2=None, op0=ALU.mult),
                            reads=["SCX", "GV"], writes=[TK[2]])
                        P.op("dve", I("scalar_tensor_tensor",
                            out=T2[:, 2064:2080], in0=scx[:, :, 1], scalar=gcol(cw + 16), in1=T2[:, 2064:2080],
                            op0=ALU.mult, op1=ALU.add), reads=["SCX", TK[2], "GV"], writes=[TK[2]])
                        P.op("dve", I("scalar_tensor_tensor",
                            out=T2[:, 2064:2080], in0=T1[:, 2064:2080], scalar=gcol(cw + 32), in1=T2[:, 2064:2080],
                            op0=ALU.mult, op1=ALU.add), reads=[TK[1], TK[2], "GV"], writes=[TK[2]])

                        def ev_bg(bi, b, n):
                            t0 = BLK[bi][0]
                            P.op("dve", I("tensor_tensor", out=T0[:, 16 + t0:16 + t0 + n], in0=PSA[:, b, 0:n],
                                                                  in1=T2[:, 16 + t0:16 + t0 + n], op=ALU.mult),
                                 reads=psk(b) + [TK[2]], writes=[TK[0]])
                        proj(wbg, wbgk, jj * 128, 8, H, "H", ev_bg)

                        def ev_g(bi, b, n, yc=yc):
                            t0 = BLK[bi][0]
                            P.op("act", I("activation", out=YA[:, yc, t0:t0 + n], in_=PSA[:, b, 0:n], func=AF.Silu),
                                 reads=psk(b), writes=[("YA", yc, bi)])
                            P.op("dve", I("tensor_tensor", out=YA[:, yc, t0:t0 + n], in0=YA[:, yc, t0:t0 + n],
                                                                  in1=T0[:, 16 + t0:16 + t0 + n], op=ALU.mult),
                                 reads=[("YA", yc, bi), TK[0]], writes=[("YA", yc, bi)])
                        proj(wgg, wggk, jj * 128, 8, H, "H", ev_g)
                out_proj(w_out_c_d[j], half * D, YA, "YA")
            P.op("pe", I("transpose", out=PSA[0:32, 4, 0:128], in_=CXL[:].rearrange("p r f -> p (r f)"),
                                             identity=ident[:]), reads=["CXL", "ident"], writes=psk(4))
            P.op("dve", I("tensor_copy", out=OST[0:32, 0:128], in_=PSA[0:32, 4, 0:128]), reads=psk(4), writes=["OST"])
            out_toks.append(P.dma("sp", I("dma_start",
                out=ncp_d[j].rearrange("r (fc c) -> (r fc) c", c=128), in_=OST[0:32, 0:128]), reads=["OST"]))
            for q in range(4):
                P.group("pe", [I("transpose", out=PSA[0:16, 4 + q, k * 128:(k + 1) * 128],
                                                              in_=CXS[:, q * 4 + k, :], identity=ident[:])
                               for k in range(4)], reads=["CXS", "ident"], writes=psk(4 + q))
            cst = T0[0:16, 0:2048]
            P.op("dve", I("tensor_copy", out=cst.rearrange("p (a b) -> p a b", a=4), in_=PSA[0:16, 4:8, :]),
                 reads=psk(4, 5, 6, 7), writes=[TK[0]])
            out_toks.append(P.dma("sp", I("dma_start", out=ncs_d[j][:, 1, :], in_=cst), reads=[TK[0]]))

        def att_mem_pre(i):
            mnt = TBf[:, 0:2048].rearrange("p (k m) -> p k m", k=8)
            memst = T2[:, 0:2048].rearrange("p (a b) -> p a b", a=2)
            P.dma("sp", I("dma_start", out=memst, in_=mem_d.rearrange("(a p) f -> p a f", p=128)), writes=[TK[2]])
            for a in range(2):
                P.op("act", I("activation", out=TB[:, 1, 0:1024], in_=memst[:, a, :], func=AF.Square,
                                                        accum_out=VST[:, a:a + 1]),
                     reads=[TK[2]], writes=[TBALL, [("VST", 0), ("VST", 1)]])
            P.op("act", I("activation", out=VST[:], in_=VST[:], func=AF.Sqrt, bias=EPSC[:, 0:1], scale=1.0 / D),
                 reads=[[("VST", 0), ("VST", 1)], "EPSC"], writes=[[("VST", 0), ("VST", 1)]])
            P.op("dve", I("reciprocal", out=VST[:], in_=VST[:]), reads=[[("VST", 0), ("VST", 1)]], writes=[[("VST", 0), ("VST", 1)]])
            for a in range(2):
                P.op("dve", I("tensor_scalar", out=memst[:, a, :], in0=memst[:, a, :], scalar1=VST[:, a:a + 1],
                                                           scalar2=None, op0=ALU.mult),
                     reads=[TK[2], [("VST", 0), ("VST", 1)]], writes=[TK[2]])

        def att_mem_pe(i):
            mnt = TBf[:, 0:2048].rearrange("p (k m) -> p k m", k=8)
            memst = T2[:, 0:2048].rearrange("p (a b) -> p a b", a=2)
            for kc in range(8):
                b = next_acc()
                P.group("pe", [I("transpose", out=PSA[:, b, a * 128:(a + 1) * 128],
                                                                      in_=memst[:, a, kc * 128:(kc + 1) * 128],
                                                                      identity=ident[:]) for a in range(2)],
                        reads=[TK[2], "ident"], writes=psk(b))
                P.op("dve", I("tensor_scalar", out=mnt[:, kc, :], in0=PSA[:, b, 0:256],
                                                                  scalar1=gcol(64 + i * 8 + kc), scalar2=None, op0=ALU.mult),
                     reads=psk(b) + ["GV"], writes=[TBALL])
            for which, (wd, od) in enumerate([(wk_d, nmk_d), (wv_d, nmv_d)]):
                for q in range(4):
                    wv_, wk_ = wload(wcols(wd[i], 0, 8, q * 256, 256))
                    for a in range(2):
                        b = next_acc()
                        P.group("pe", [I("matmul",
                            PSA[:, b, 0:256], lhsT=mnt[:, kc, a * 128:(a + 1) * 128], rhs=wv_[:, kc, :],
                            start=(kc == 0), stop=(kc == 7)) for kc in range(8)],
                            reads=[TBALL, wk_], writes=psk(b))
                        stg = T0[:, (a * 4 + q) * 256:(a * 4 + q + 1) * 256]
                        P.op("act", I("activation", out=stg, in_=PSA[:, b, 0:256], func=AF.Copy),
                             reads=psk(b), writes=[TK[0]])
                        if which == 1:
                            P.op("dve", I("tensor_copy", out=VM[:, a, q * 256:(q + 1) * 256],
                                                                              in_=PSA[:, b, 0:256]),
                                 reads=psk(b), writes=["VM"])
                    if which == 0:
                        for jj in range(2):
                            b = next_acc()
                            dc = q * 2 + jj
                            P.group("pe", [I("matmul",
                                PSA[:, b, 0:256], lhsT=wv_[:, kc, jj * 128:(jj + 1) * 128], rhs=mnt[:, kc, :],
                                start=(kc == 0), stop=(kc == 7)) for kc in range(8)],
                                reads=[TBALL, wk_], writes=psk(b))
                            P.op("act", I("activation", out=KT[:, dc, :], in_=PSA[:, b, 0:256],
                                                                          func=AF.Copy), reads=psk(b), writes=["KT"])
                for a in range(2):
                    out_toks.append(P.dma("sp", I("dma_start", out=od[i][a * 128:(a + 1) * 128, :],
                                                  in_=T0[:, a * 1024:(a + 1) * 1024]), reads=[TK[0]]))


        def attention(i):
            P.mute = not DBG.get("a_q", True)
            rms_norm_fm(32 + i * 8, lambda kc, t0, n: H[:, kc, t0:t0 + n], "H")

            for pr in range(4):
                wq, wqk = wload(wcols(wq_d[i], 0, 8, pr * 256, 256))
                for jj in range(2):
                    dc = pr * 2 + jj

                    def ev_q(bi, b, n, dc=dc):
                        t0 = BLK[bi][0]
                        P.op("act", I("activation", out=YA[:, dc, t0:t0 + n], in_=PSA[:, b, 0:n], func=AF.Copy,
                                                           scale=0.0625), reads=psk(b), writes=[("YA", dc, bi)])
                    proj(wq, wqk, jj * 128, 8, H, "H", ev_q, blks=range(4))
                b = next_acc()
                P.group("pe", [I("matmul", PSA[0:16, b, 0:256], lhsT=H[:, kc, 2048:2064],
                                                                     rhs=wq[:, kc, :], start=(kc == 0), stop=(kc == 7))
                               for kc in range(8)], reads=[wqk] + [("H", kc, 4) for kc in range(8)], writes=psk(b))
                P.op("act", I("activation", out=QS[:, pr * 256:(pr + 1) * 256], in_=PSA[0:16, b, 0:256],
                                                               func=AF.Copy, scale=0.0625), reads=psk(b), writes=["OST"])

            P.mute = not DBG.get("a_core", True)
            kvb = [[TT[:, a_, :].bitcast(BF16)[:, k * 2048:(k + 1) * 2048].rearrange("p (a c) -> p a c", a=2)
                    for k in range(2)] for a_ in range(2)]
            ex = TB[:, 1, 0:1024].rearrange("p (a b) -> p a b", a=2)
            EXK = [("TB", 1, 0), ("TB", 1, 1)]

            def sample_pv(s):
                par = s % 2
                vsb = kvb[par][1]
                for dc in range(8):
                    P.group("pe", [I("matmul", PSA[:, 5, dc * 16 + s:dc * 16 + s + 1],
                                     lhsT=vsb[:, mt, dc * 128:(dc + 1) * 128],
                                     rhs=ES[:, s, mt, dc // 2:dc // 2 + 1], start=(mt == 0), stop=(mt == 1))
                                   for mt in range(2)],
                            reads=[TK[par][1], ("ES", s)], writes=psk(5))

            n_it = 0
            for bi in range(4):
                t0 = BLK[bi][0]
                for h in range(4):
                    s = n_it
                    par = s % 2
                    n_it += 1
                    ksb, vsb = kvb[par][0], kvb[par][1]
                    P.dma("pool", I("dma_start", out=ksb, in_=ck_d[i, s].rearrange("(a p) c -> p a c", p=128)),
                          writes=[TK[par][0]])
                    P.dma("pool", I("dma_start", out=vsb, in_=cv_d[i, s].rearrange("(a p) c -> p a c", p=128)),
                          writes=[TK[par][1]])
                    for mt in range(2):
                        P.group("pe", [I("matmul", PSA[:, mt, :], lhsT=KT[:, 2 * h + dcc, mt * 128:(mt + 1) * 128],
                                         rhs=YA[:, 2 * h + dcc, t0:t0 + 512], start=(dcc == 0), stop=(dcc == 1))
                                       for dcc in range(2)],
                                reads=["KT", ("YA", 2 * h, bi), ("YA", 2 * h + 1, bi)], writes=psk(mt))
                    for hf in range(2):
                        P.op("pe", I("matmul", PSA[:, 2 + hf, :], lhsT=SEL[:, s, :], rhs=QS[:, hf * 512:(hf + 1) * 512],
                                     start=True, stop=True), reads=["SEL", "OST"], writes=psk(2 + hf))
                    if s > 0:
                        sample_pv(s - 1)
                    P.op("act", I("activation", out=ex, in_=PSA[:, 0:2, :], func=AF.Exp), reads=psk(0, 1), writes=[EXK])
                    for hf in range(2):
                        for mt in range(2):
                            for hh in range(2):
                                hd = hf * 2 + hh
                                P.op("dve", I("scalar_tensor_tensor", out=TB[:, 0, 0:256],
                                              in0=ksb[:, mt, hd * 256:(hd + 1) * 256], scalar=1.0,
                                              in1=PSA[:, 2 + hf, hh * 256:hh * 256 + 256], op0=ALU.mult, op1=ALU.mult,
                                              accum_out=SC[:, s, mt, hd:hd + 1]),
                                     reads=[TK[par][0]] + psk(2 + hf), writes=[("TB", 0, 0), ("SC", s)])
                    P.group("pe", [I("matmul", PSA[:, 4, :], lhsT=onesb[:], rhs=ex[:, mt, :],
                                     start=(mt == 0), stop=(mt == 1)) for mt in range(2)],
                            reads=[EXK, "onesb"], writes=psk(4))
                    for dcc in range(2):
                        P.group("pe", [I("matmul", PSA[:, 6 + dcc, :], lhsT=VM[:, mt, (2 * h + dcc) * 128:(2 * h + dcc + 1) * 128],
                                         rhs=ex[:, mt, :], start=(mt == 0), stop=(mt == 1)) for mt in range(2)],
                                reads=[EXK, "VM"], writes=psk(6 + dcc))
                    P.op("act", I("activation", out=RS[:], in_=PSA[:, 4, :], func=AF.Ln), reads=psk(4), writes=[RSK])
                    P.op("act", I("activation", out=RS[:], in_=RS[:], func=AF.Exp, scale=-1.0), reads=[RSK], writes=[RSK])
                    for dcc in range(2):
                        P.op("dve", I("tensor_tensor", out=YA[:, 2 * h + dcc, t0:t0 + 512],
                                                                           in0=PSA[:, 6 + dcc, :], in1=RS[:], op=ALU.mult),
                             reads=psk(6 + dcc) + [RSK], writes=[("YA", 2 * h + dcc, bi)])
                    P.op("act", I("activation", out=ES[:, s], in_=SC[:, s], func=AF.Exp),
                         reads=[("SC", s)], writes=[("ES", s)])
            sample_pv(NS - 1)
            P.mute = not DBG.get("a_samp", True)
            P.op("pe", I("matmul", PSA[:, 4, 0:128], lhsT=onesb[:], rhs=ES[:].rearrange("p s m h -> p (s m h)"),
                                          start=True, stop=True), reads=[("ES", s) for s in range(NS)] + ["onesb"],
                 writes=psk(4))
            dv = PSA[:, 4, 0:128].rearrange("p (s m h) -> p s m h", s=16, m=2)
            P.op("dve", I("tensor_copy", out=DS[:], in_=dv[:, :, 0, :]), reads=psk(4), writes=["DS"])
            P.op("dve", I("tensor_tensor", out=DS[:], in0=dv[:, :, 1, :], in1=DS[:], op=ALU.add),
                 reads=psk(4) + ["DS"], writes=["DS"])
            P.op("dve", I("reciprocal", out=DS[:], in_=DS[:]), reads=["DS"], writes=["DS"])
            for dc in range(8):
                P.op("dve", I("tensor_tensor", out=YA[:, dc, 2048:2064], in0=PSA[:, 5, dc * 16:(dc + 1) * 16],
                                                             in1=DS[:, :, dc // 2], op=ALU.mult),
                     reads=psk(5) + ["DS"], writes=[("YA", dc, 4)])
            P.mute = not DBG.get("a_out", True)
            hooks = None
            if i + 1 < DBG["layers"]:
                hooks = {0: (lambda: att_mem_pre(i + 1)), 2: (lambda: att_mem_pe(i + 1))}
            out_proj(wo_d[i], 0, YA, "YA", hooks)
            P.mute = False

        if DBG["att"] and DBG["layers"] > 0:
            att_mem_pre(0)
            att_mem_pe(0)
        for i in range(DBG["layers"]):
            if DBG["mix"]:
                if i % 2 == 0:
                    even_layer(i)
                else:
                    odd_layer(i)
            if DBG["att"]:
                attention(i)

        yst = TT[:, 0:2, :].rearrange("p a b -> p (a b)")[:, 0:4096].rearrange("p (k n) -> p k n", k=8)
        for bi in range(5):
            t0, n = BLK[bi]
            P.op("act", I("activation", out=TBf[:, 0:8 * n].rearrange("p (k n) -> p k n", k=8),
                                                           in_=XP[:, :, t0:t0 + n], func=AF.Square),
                 reads=[("XP", kc, bi) for kc in range(8)], writes=[TBALL])
            P.group("pe", [I("matmul", PSA[:, 4, 0:n], lhsT=onesb[:], rhs=TBf[:, kc * n:(kc + 1) * n],
                                                          start=(kc == 0), stop=(kc == 7)) for kc in range(8)],
                    reads=[TBALL, "onesb"], writes=psk(4))
            P.op("act", I("activation", out=RS[:, 0:n], in_=PSA[:, 4, 0:n], func=AF.Ln,
                                                    bias=EPSC[:, 0:1], scale=1.0 / D), reads=psk(4) + ["EPSC"], writes=[RSK])
            P.op("act", I("activation", out=RS[:, 0:n], in_=RS[:, 0:n], func=AF.Exp, scale=-0.5), reads=[RSK], writes=[RSK])
            for kc in range(8):
                P.op("dve", I("scalar_tensor_tensor",
                    out=yst[:, kc, 0:n], in0=XP[:, kc, t0:t0 + n], scalar=gcol(112 + kc), in1=RS[:, 0:n],
                    op0=ALU.mult, op1=ALU.mult), reads=[("XP", kc, bi), RSK, "GV"], writes=[TK[0], TK[1]])
            ntile = 4 if bi < 4 else 1
            m = 128 if bi < 4 else NS
            for tl in range(ntile):
                for hf in range(2):
                    b = next_acc()
                    P.group("pe", [I("transpose",
                        out=PSA[0:m, b, k * 128:(k + 1) * 128], in_=yst[:, hf * 4 + k, tl * 128:tl * 128 + m],
                        identity=ident[:]) for k in range(4)], reads=[TK[0], TK[1], "ident"], writes=psk(b))
                    o0 = (tl % 2) * 1024 + hf * 512
                    P.op("act" if hf else "dve",
                         (I("activation", out=T2[0:m, o0:o0 + 512], in_=PSA[0:m, b, :],
                                                                   func=AF.Copy)) if hf else
                         (I("tensor_copy", out=T2[0:m, o0:o0 + 512], in_=PSA[0:m, b, :])),
                         reads=psk(b), writes=[TK[2][tl % 2]])
                if bi < 4:
                    r0 = t0 + tl * 128
                    out_toks.append(P.dma("sp", I("dma_start", out=yp_d[r0:r0 + 128, :], in_=T2[:, (tl % 2) * 1024:(tl % 2) * 1024 + 1024]),
                                          reads=[TK[2][tl % 2]]))
                else:
                    out_toks.append(P.dma("sp", I("dma_start", out=ys_d, in_=T2[0:16, 0:1024]), reads=[TK[2]]))

        P.wait("sp", out_toks)
        P.replay(st)
    return nc


_CACHE = {}

W_NAMES = ["norm_mix_g", "norm_xattn_g", "norm_mem_g", "w_in_ab", "pool_maps", "pool_scale", "sgu_w", "sgu_b",
           "sgu_g", "w_out_ab", "w_in_c", "conv_w", "w_out_c", "w_q", "w_k", "w_v", "w_o", "norm_final_g"]


def kernel(**inputs):
    f = lambda a: np.ascontiguousarray(np.asarray(a, dtype=np.float32))
    if "nc" not in _CACHE:
        _CACHE["nc"] = build_program()
    nc = _CACHE["nc"]
    shared = {k: f(inputs[k]) for k in W_NAMES}
    xp = np.asarray(inputs["x_prompt"]); xs = np.asarray(inputs["x_sample"])
    mem = np.asarray(inputs["mem_prompt"]); sp = np.asarray(inputs["state_pool"])
    scv = np.asarray(inputs["state_conv"]); ck = np.asarray(inputs["cache_mem_k"]); cv = np.asarray(inputs["cache_mem_v"])
    in_maps = []
    for c in range(NCORES):
        sl = slice(c * NS, (c + 1) * NS)
        m = dict(shared)
        m["xp"] = f(xp[c]); m["xs"] = f(xs[sl, 0]); m["mem"] = f(mem[c])
        m["spool"] = f(sp[:, sl]); m["sconv"] = f(scv[:, sl])
        m["ck"] = f(ck[:, sl].reshape(DEPTH, NS, 256, D)); m["cv"] = f(cv[:, sl].reshape(DEPTH, NS, 256, D))
        in_maps.append(m)
    res = run_bass_kernel_spmd(nc, in_maps, core_ids=list(range(NCORES)))
    R = res.results
    cat = lambda k, ax: np.concatenate([np.expand_dims(r[k], ax) if False else r[k] for r in R], axis=ax)
    y_prompt = np.stack([r["yp"] for r in R], 0)
    y_sample = np.concatenate([r["ys"] for r in R], 0)[:, None, :]
    npp = np.stack([r["npp"] for r in R], 1)
    nps = np.concatenate([r["nps"] for r in R], 1)
    ncp = np.stack([r["ncp"] for r in R], 1)
    ncs = np.concatenate([r["ncs"] for r in R], 1)
    nsv = np.concatenate([r["nsv"] for r in R], 1)[:, :, None, :]
    nmk = np.stack([r["nmk"] for r in R], 1).reshape(DEPTH, NCORES, 256, 4, 256)
    nmv = np.stack([r["nmv"] for r in R], 1).reshape(DEPTH, NCORES, 256, 4, 256)
    return (y_prompt, y_sample, npp, nps, ncp, ncs, nsv, nmk, nmv)
```

```python
from contextlib import ExitStack
import numpy as np
import concourse.bass as bass
import concourse.mybir as mybir
from concourse.bass_utils import run_bass_kernel_spmd

F32 = mybir.dt.float32
BF16 = mybir.dt.bfloat16
I32 = mybir.dt.int32
AF = mybir.ActivationFunctionType
ALU = mybir.AluOpType
AX = mybir.AxisListType

ENGS = ("pe", "act", "dve", "pool", "sp")
NCORES = 8
D = 1024
SEQ = 2048
NS = 16
NT = SEQ + NS
DEPTH = 4
EPS = 1e-6
BLK = [(0, 512), (512, 512), (1024, 512), (1536, 512), (2048, NS)]
POOL_W = (2, 4, 8, 16)
DBG = {"layers": DEPTH, "mix": True, "att": True}


class Prog:
    def __init__(self, nc, n_dma_sems=40):
        self.nc = nc
        self.ops = {e: [] for e in ENGS}
        self.cnt = {e: 0 for e in ENGS}
        self.waited = {e: {} for e in ENGS}
        self.last_w = {}
        self.readers = {}
        self.n_dma_sems = n_dma_sems
        self.dma_rr = {e: 0 for e in ENGS}
        self.dma_val = [0] * n_dma_sems
        self.dma_last_tok = [None] * n_dma_sems

    def _need(self, eng, toks):
        best = {}
        for t in toks:
            if t is None:
                continue
            k, v, _ = t
            if v > best.get(k, 0):
                best[k] = v
        out = []
        for k, v in best.items():
            if self.waited[eng].get(k, 0) >= v:
                continue
            self.waited[eng][k] = v
            out.append((k, v))
        return out

    @staticmethod
    def _flat(keys):
        out = []
        for k in keys:
            if isinstance(k, list):
                out.extend(Prog._flat(k))
            else:
                out.append(k)
        return out

    @staticmethod
    def _excl(reads, writes):
        reads = Prog._flat(reads); writes = Prog._flat(writes)
        ps = [k for k in reads if isinstance(k, tuple) and k[0] == "PS"]
        if ps:
            reads = [k for k in reads if not (isinstance(k, tuple) and k[0] == "PS")]
            writes = writes + [k for k in ps if k not in writes]
        return reads, writes

    def _deps(self, eng, reads, writes, extra):
        reads, writes = self._excl(reads, writes)
        toks = list(extra)
        for r in reads:
            toks.append(self.last_w.get(r))
        for w in writes:
            toks.append(self.last_w.get(w))
            for t in self.readers.get(w, ()):
                toks.append(t)
        return self._need(eng, toks)

    def _reg(self, tok, reads, writes):
        reads, writes = self._excl(reads, writes)
        for r in reads:
            self.readers.setdefault(r, []).append(tok)
        for w in writes:
            self.last_w[w] = tok
            self.readers[w] = []

    mute = False

    def group(self, eng, fns, reads=(), writes=(), extra=()):
        if self.mute:
            return None
        waits = self._deps(eng, reads, writes, extra)
        self.cnt[eng] += 1
        tok = (eng, self.cnt[eng], eng)
        n = len(fns)
        for i, fn in enumerate(fns):
            self.ops[eng].append((waits if i == 0 else (), fn, ("eng", eng) if i == n - 1 else None))
        self._reg(tok, reads, writes)
        return tok

    def op(self, eng, fn, reads=(), writes=(), extra=()):
        return self.group(eng, [fn], reads, writes, extra)

    def dma(self, q, fn, reads=(), writes=(), extra=()):
        if self.mute:
            return None
        lo, hi = (0, 24) if q != "pool" else (24, self.n_dma_sems)
        i = lo + self.dma_rr[q] % (hi - lo)
        self.dma_rr[q] += 1
        reads = self._flat(reads); writes = self._flat(writes)
        toks = list(extra)
        toks.append(self.dma_last_tok[i])
        for r in reads:
            toks.append(self.last_w.get(r))
        for w in writes:
            toks.append(self.last_w.get(w))
            toks.extend(self.readers.get(w, ()))
        waits = self._need(q, toks)
        self.dma_val[i] += 16
        tok = (("dma", i), self.dma_val[i], None)
        self.dma_last_tok[i] = tok
        self.ops[q].append((waits, fn, ("dma", i)))
        self._reg(tok, reads, writes)
        return tok

    def wait(self, eng, toks):
        waits = self._need(eng, toks)
        if waits:
            self.ops[eng].append((waits, None, None))

    def replay(self, stack):
        nc = self.nc
        semh = {}
        for e in ENGS:
            semh[e] = stack.enter_context(nc.semaphore("s_" + e))
        for i in range(self.n_dma_sems):
            semh[("dma", i)] = stack.enter_context(nc.semaphore("s_dma%d" % i))
        block = stack.enter_context(nc.Block())
        hmap = {"pe": block.tensor, "act": block.scalar, "dve": block.vector,
                "pool": block.gpsimd, "sp": block.sync}

        def mk(e):
            def body(eng):
                for waits, fn, sig in self.ops[e]:
                    for k, v in waits:
                        eng.wait_ge(semh[k], v)
                    if fn is None:
                        continue
                    ins = fn(eng)
                    if sig is not None:
                        if sig[0] == "eng":
                            ins.then_inc(semh[sig[1]], 1)
                        else:
                            ins.then_inc(semh[sig], 16)
            return body

        for e in ENGS:
            hmap[e](mk(e))


def I(method, *args, **kw):
    return lambda e: getattr(e, method)(*args, **kw)


def build_program():
    nc = bass.Bass("TRN2", target_bir_lowering=False)

    def din(name, shape):
        return nc.dram_tensor(name, list(shape), F32, kind="ExternalInput").ap()

    def dout(name, shape):
        return nc.dram_tensor(name, list(shape), F32, kind="ExternalOutput").ap()

    xp_d = din("xp", [SEQ, D]); xs_d = din("xs", [NS, D]); mem_d = din("mem", [256, D])
    spool_d = din("spool", [2, NS, 15, D]); sconv_d = din("sconv", [2, NS, 2, 2 * D])
    ck_d = din("ck", [DEPTH, NS, 256, D]); cv_d = din("cv", [DEPTH, NS, 256, D])
    g_mix_d = din("norm_mix_g", [DEPTH, D]); g_xa_d = din("norm_xattn_g", [DEPTH, D])
    g_mem_d = din("norm_mem_g", [DEPTH, D])
    w_in_ab_d = din("w_in_ab", [2, D, 5 * D]); pmaps_d = din("pool_maps", [2, 4, 256, 256])
    pscale_d = din("pool_scale", [2, D]); sgu_w_d = din("sgu_w", [2, 4, 128, 128])
    sgu_b_d = din("sgu_b", [2, 4, 128]); sgu_g_d = din("sgu_g", [2, D])
    w_out_ab_d = din("w_out_ab", [2, 2 * D, D]); w_in_c_d = din("w_in_c", [2, D, 8 * D])
    conv_w_d = din("conv_w", [2, 3, 2 * D]); w_out_c_d = din("w_out_c", [2, 2 * D, D])
    wq_d = din("w_q", [DEPTH, D, D]); wk_d = din("w_k", [DEPTH, D, D])
    wv_d = din("w_v", [DEPTH, D, D]); wo_d = din("w_o", [DEPTH, D, D])
    g_fin_d = din("norm_final_g", [D])

    yp_d = dout("yp", [SEQ, D]); ys_d = dout("ys", [NS, D])
    npp_d = dout("npp", [2, 15, D]); nps_d = dout("nps", [2, NS, 15, D])
    ncp_d = dout("ncp", [2, 2, 2 * D]); ncs_d = dout("ncs", [2, NS, 2, 2 * D])
    nsv_d = dout("nsv", [2, NS, D]); nmk_d = dout("nmk", [DEPTH, 256, D]); nmv_d = dout("nmv", [DEPTH, 256, D])

    P = Prog(nc)
    out_toks = []
    with ExitStack() as st:
        def sb(name, shape, dt):
            return st.enter_context(nc.sbuf_tensor(name, list(shape), dt))

        XP = sb("XP", [128, 8, NT], F32)
        H = sb("H", [128, 8, NT], BF16)
        YA = sb("YA", [128, 8, NT], BF16)
        NW = 4
        WB = sb("WB", [128, NW, 2048], BF16)
        TT = sb("TT", [128, 3, 2080], F32)
        TB = sb("TB", [128, 2, NT], BF16)
        KT = sb("KT", [128, 8, 256], BF16)
        VM = sb("VM", [128, 2, 1024], BF16)
        GV = sb("GV", [128, 216], F32)
        OST = sb("OST", [32, 1024], F32)
        RS = sb("RS", [128, 512], F32)
        ident = sb("ident", [128, 128], F32)
        onesf = sb("onesf", [128, 128], F32)
        onesb = sb("onesb", [128, 128], BF16)
        SEL = sb("SEL", [16, 16, 128], BF16)
        INVT = sb("INVT", [128, 4, 16], F32)
        IOT = sb("IOT", [128, 16], I32)
        WT = sb("WT", [128, 4, 128], BF16)
        WSC = sb("WSC", [16, 4], F32)
        DG = sb("DG", [16, 4, 16], BF16)
        XAS = sb("XAS", [128, 8, NS], F32)
        XAL = sb("XAL", [128, 8, 32], F32)
        CSUM = sb("CSUM", [128, 8, NS], F32)
        SM1 = sb("SM1", [128, 64], F32)
        SM2 = sb("SM2", [128, 64], F32)
        SCX = sb("SCX", [128, 16, 32], F32)
        CXL = sb("CXL", [128, 2, 16], F32)
        CXS = sb("CXS", [128, 16, NS], F32)
        SC = sb("SC", [128, 16, 2, 4], F32)
        ES = sb("ES", [128, 16, 2, 4], BF16)
        DS = sb("DS", [128, 16, 4], F32)
        VST = sb("VST", [128, 2], F32)
        PSA = st.enter_context(nc.psum_tensor("PSA", [128, 8, 512], F32))

        QS = OST[0:16, 0:512].bitcast(BF16)
        T0, T1, T2 = TT[:, 0, :], TT[:, 1, :], TT[:, 2, :]
        GVT = TT[:, 2, 0:1024]
        BS8 = TT[:, 2, 1024:2048].rearrange("p (c t) -> p c t", c=8)
        TBf = TB[:].rearrange("p a b -> p (a b)")
        TK = [["T0a", "T0b"], ["T1a", "T1b"], ["T2a", "T2b"]]
        TBALL = [("TB", r, b_) for r in range(2) for b_ in range(5)]

        def bank(b):
            return PSA[:, b, :]

        def psk(*bs):
            return [("PS", b) for b in bs]

        P.op("pool", I("memset", onesf[:], 1.0), writes=["onesf"])
        P.op("pool", I("memset", onesb[:], 1.0), writes=["onesb"])
        P.op("pool", I("affine_select", out=ident[:], in_=onesf[:], pattern=[[-1, 128]],
                                               compare_op=ALU.is_equal, fill=0.0, base=0,
                                               channel_multiplier=1),
             reads=["onesf"], writes=["ident"])
        P.op("pool", I("memset", SEL[:], 1.0), writes=["SEL"])
        P.op("pool", I("affine_select", out=SEL[:], in_=SEL[:], pattern=[[-1, 16], [0, 128]],
                                               compare_op=ALU.is_equal, fill=0.0, base=0,
                                               channel_multiplier=1),
             reads=["SEL"], writes=["SEL"])
        P.op("pool", I("iota", IOT[:], pattern=[[1, 16]], base=1, channel_multiplier=0),
             writes=["IOT"])
        P.op("dve", I("tensor_copy", out=SM1[:, 0:16], in_=IOT[:]), reads=["IOT"], writes=["SM1"])
        for g, w in enumerate(POOL_W):
            P.op("dve", I("tensor_scalar", out=INVT[:, g, :], in0=SM1[:, 0:16],
                                                           scalar1=float(w), scalar2=None, op0=ALU.min),
                 reads=["SM1"], writes=["INVT"])
        P.op("dve", I("reciprocal", out=INVT[:], in_=INVT[:]), reads=["INVT"], writes=["INVT"])

        ga = T0[0:120, 0:128]
        gb_ = T0[0:96, 128:256]
        for k, (src, r0) in enumerate([(g_mix_d, 0), (g_xa_d, 32), (g_mem_d, 64)]):
            P.dma("sp", I("dma_start",
                out=T0[r0:r0 + 32, 0:128], in_=src.rearrange("i (kc p) -> (i kc) p", p=128)), writes=[TK[0]])
        P.dma("sp", I("dma_start", out=T0[96:112, 0:128],
                                          in_=pscale_d.rearrange("j (kc p) -> (j kc) p", p=128)), writes=[TK[0]])
        P.dma("sp", I("dma_start", out=T0[112:120, 0:128],
                                          in_=g_fin_d.rearrange("(kc p) -> kc p", p=128)), writes=[TK[0]])
        P.dma("sp", I("dma_start", out=T0[0:96, 128:256],
                                          in_=conv_w_d.rearrange("j k (fc p) -> (j k fc) p", p=128)), writes=[TK[0]])
        P.group("pe", [I("transpose", out=PSA[:, 4, 0:120], in_=ga, identity=ident[0:120, 0:120]),
                       I("transpose", out=PSA[:, 4, 128:224], in_=gb_, identity=ident[0:96, 0:96])],
                reads=[TK[0], "ident"], writes=psk(4))
        P.op("dve", I("tensor_copy", out=GV[:, 0:120], in_=PSA[:, 4, 0:120]), reads=psk(4), writes=["GV"])
        P.op("dve", I("tensor_copy", out=GV[:, 120:216], in_=PSA[:, 4, 128:224]), reads=psk(4), writes=["GV"])

        def gcol(c):
            return GV[:, c:c + 1]

        wstate = {"slot": 0}

        def wload(src3):
            s = wstate["slot"]
            wstate["slot"] = (s + 1) % NW
            a, b = src3.shape[1], src3.shape[2]
            dst = WB[:, s, 0:a * b].rearrange("p (a b) -> p a b", a=a)
            P.dma("pool", I("dma_start", out=dst, in_=src3), writes=[("W", s)])
            return dst, ("W", s)

        def wcols(wd, r0, nk, c0, ncol):
            return wd[r0:r0 + nk * 128, c0:c0 + ncol].rearrange("(kc p) c -> p kc c", p=128)

        acc = {"b": 0}

        def next_acc():
            b = acc["b"]
            acc["b"] = (b + 1) % 8
            return b

        def proj(wv, wkey, col0, nk, src, srckey, evac, blks=range(5)):
            for bi in blks:
                t0, n = BLK[bi]
                b = next_acc()
                fns = []
                for kc in range(nk):
                    fns.append(I("matmul",
                        PSA[:, b, 0:n], lhsT=wv[:, kc, col0:col0 + 128], rhs=src[:, kc, t0:t0 + n],
                        start=(kc == 0), stop=(kc == nk - 1)))
                P.group("pe", fns, reads=[wkey] + [(srckey, kc, bi) for kc in range(nk)], writes=psk(b))
                evac(bi, b, n)

        SQK = [[("TB", 0, b_) for b_ in range(4)], [("TB", 0, 4)] + [("TB", 1, b_) for b_ in range(4)]]
        RSK = [("RS", 0), ("RS", 1)]
        nrm = {"k": 0}

        def rms_norm_fm(gbase, dst, dstkey, fp32_out=False):
            subs = []
            for bi in range(5):
                t0b, nb = BLK[bi]
                for t0 in range(t0b, t0b + nb, 256):
                    subs.append((bi, t0, min(256, t0b + nb - t0), nrm["k"] % 2))
                    nrm["k"] += 1

            def st_a(bi, t0, n, p):
                sq = TBf[:, p * 2048:p * 2048 + 8 * n].rearrange("p (k n) -> p k n", k=8)
                P.op("act", I("activation", out=sq, in_=XP[:, :, t0:t0 + n], func=AF.Square),
                     reads=[("XP", kc, bi) for kc in range(8)], writes=[SQK[p]])
                P.group("pe", [I("matmul", PSA[:, 4 + p, 0:n], lhsT=onesb[:], rhs=sq[:, kc, :],
                                 start=(kc == 0), stop=(kc == 7)) for kc in range(8)],
                        reads=[SQK[p], "onesb"], writes=psk(4 + p))

            def st_b(bi, t0, n, p):
                rs = RS[:, p * 256:p * 256 + n]
                P.op("act", I("activation", out=rs, in_=PSA[:, 4 + p, 0:n], func=AF.Ln, bias=EPSC[:, 0:1], scale=1.0 / D),
                     reads=psk(4 + p) + ["EPSC"], writes=[RSK[p]])
                P.op("act", I("activation", out=rs, in_=rs, func=AF.Exp, scale=-0.5), reads=[RSK[p]], writes=[RSK[p]])
                for kc in range(8):
                    P.op("dve", I("scalar_tensor_tensor", out=dst(kc, t0, n), in0=XP[:, kc, t0:t0 + n],
                                  scalar=gcol(gbase + kc), in1=rs, op0=ALU.mult, op1=ALU.mult),
                         reads=[("XP", kc, bi), RSK[p], "GV"], writes=[(dstkey, kc, bi)])

            st_a(*subs[0])
            for k in range(1, len(subs)):
                st_a(*subs[k])
                st_b(*subs[k - 1])
            st_b(*subs[-1])

        EPSC = sb("EPSC", [128, 1], F32)
        P.op("pool", I("memset", EPSC[:], EPS), writes=["EPSC"])

        def resid_add(fc, srckey_unused=None):
            def evac(bi, b, n):
                t0 = BLK[bi][0]
                P.op("dve", I("tensor_tensor", out=XP[:, fc, t0:t0 + n], in0=PSA[:, b, 0:n],
                                                      in1=XP[:, fc, t0:t0 + n], op=ALU.add),
                     reads=psk(b) + [("XP", fc, bi)], writes=[("XP", fc, bi)])
            return evac

        def out_proj(wd, r0, src, srckey, hooks=None):
            for pr in range(4):
                if hooks and pr in hooks:
                    hooks[pr]()
                wv, wkey = wload(wcols(wd, r0, 8, pr * 256, 256))
                for jj in range(2):
                    fc = pr * 2 + jj
                    proj(wv, wkey, jj * 128, 8, src, srckey, resid_add(fc))

        for q8 in range(8):
            xst = TT[:, q8 % 2, 0:2048].rearrange("p (a b) -> p a b", a=2)
            xk = TK[q8 % 2]
            P.dma("sp", I("dma_start", out=xst, in_=xp_d[q8 * 256:(q8 + 1) * 256, :].rearrange("(t p) f -> p t f", p=128)),
                  writes=[xk])
            for kc in range(8):
                b = next_acc()
                P.group("pe", [I("transpose", out=PSA[:, b, t * 128:(t + 1) * 128], in_=xst[:, t, kc * 128:(kc + 1) * 128],
                                 identity=ident[:]) for t in range(2)], reads=[xk, "ident"], writes=psk(b))
                if kc % 2:
                    P.op("act", I("activation", out=XP[:, kc, q8 * 256:(q8 + 1) * 256], in_=PSA[:, b, 0:256], func=AF.Copy),
                         reads=psk(b), writes=[("XP", kc, q8 // 2)])
                else:
                    P.op("dve", I("tensor_copy", out=XP[:, kc, q8 * 256:(q8 + 1) * 256], in_=PSA[:, b, 0:256]),
                         reads=psk(b), writes=[("XP", kc, q8 // 2)])
        P.dma("sp", I("dma_start", out=OST[0:16, :], in_=xs_d), writes=["OST"])
        b = next_acc()
        P.group("pe", [I("transpose", out=PSA[:, b, kc * 16:(kc + 1) * 16],
                                                         in_=OST[0:16, kc * 128:(kc + 1) * 128],
                                                         identity=ident[0:16, 0:16]) for kc in range(8)],
                reads=["OST", "ident"], writes=psk(b))
        P.op("dve", I("tensor_copy", out=XP[:, :, 2048:2064],
                                                 in_=PSA[:, b, 0:128].rearrange("p (k s) -> p k s", k=8)),
             reads=psk(b), writes=[("XP", kc, 4) for kc in range(8)])

        def even_layer(i):
            j = i // 2
            wab = w_in_ab_d[j]
            P.dma("sp", I("dma_start", out=GVT, in_=sgu_g_d[j].partition_broadcast(128)), writes=[TK[2]])
            for rep in range(2):
                P.dma("sp", I("dma_start",
                    out=BS8.rearrange("p (g two) t -> p g two t", two=2)[:, :, rep, :],
                    in_=sgu_b_d[j].rearrange("g t -> (g t)").partition_broadcast(128).rearrange("p (g t) -> p g t", g=4)),
                    writes=[TK[2]])
            P.dma("sp", I("dma_start", out=WSC[:], in_=sgu_w_d[j][:, 0, 0].partition_broadcast(16), allow_slow_non_contiguous=True),
                  writes=["WSC"])
            wst = T1[:, 0:512].rearrange("p (g s) -> p g s", g=4)
            P.dma("sp", I("dma_start", out=wst, in_=sgu_w_d[j].rearrange("g t s -> t g s")), writes=[TK[1]])
            P.group("pe", [I("transpose", out=PSA[:, 5, g * 128:(g + 1) * 128], in_=wst[:, g, :],
                                                      identity=ident[:]) for g in range(4)],
                    reads=[TK[1], "ident"], writes=psk(5))
            P.op("dve", I("tensor_copy", out=T1[:, 512:1024], in_=PSA[:, 5, :]), reads=psk(5), writes=[TK[1]])
            P.op("pool", I("affine_select", out=WT[:], in_=T1[:, 512:1024].rearrange("p (g t) -> p g t", g=4),
                                                   pattern=[[0, 4], [1, 128]], compare_op=ALU.is_ge, fill=0.0,
                                                   base=0, channel_multiplier=-1),
                 reads=[TK[1]], writes=["WT"])
            for g in range(4):
                P.op("dve", I("tensor_scalar", out=DG[:, g, :], in0=ident[0:16, 0:16],
                                                           scalar1=WSC[:, g:g + 1], scalar2=None, op0=ALU.mult),
                     reads=["WSC", "ident"], writes=["DG"])
            spt = T0[0:120, 0:2048].rearrange("p (a c) -> p a c", a=2)
            P.dma("sp", I("dma_start", out=spt, in_=spool_d[j].rearrange("(a s) r c -> (s r) a c", a=2)),
                  writes=[TK[0]])
            out_toks.append(P.dma("sp", I("dma_start", out=nps_d[j][:, 0:14, :], in_=spool_d[j][:, 1:15, :])))
            for a in range(2):
                for cc in range(8):
                    b = next_acc()
                    P.op("pe", I("transpose", out=PSA[:, b, 0:120],
                                                                       in_=spt[:, a, cc * 128:(cc + 1) * 128],
                                                                       identity=ident[0:120, 0:120]),
                         reads=[TK[0], "ident"], writes=psk(b))
                    w = POOL_W[cc // 2]
                    P.op("dve", I("tensor_reduce",
                        out=CSUM[:, cc, a * 8:(a + 1) * 8],
                        in_=PSA[:, b, 0:120].rearrange("p (s r) -> p s r", s=8)[:, :, 16 - w:15],
                        axis=AX.X, op=ALU.add),
                        reads=psk(b), writes=["CSUM"])

            rms_norm_fm(i * 8, lambda kc, t0, n: H[:, kc, t0:t0 + n], "H")

            wvs = [wload(wcols(wab, 0, 8, 3 * D + q * 256, 256)) for q in range(4)]
            jk = TT[:, 0, :].bitcast(BF16)[:, 0:1024].rearrange("p (a b) -> p a b", a=2)

            def e1_cfg(ti):
                t0, m = (ti * 128, 128) if ti < 16 else (2048, NS)
                bi = ti // 4 if ti < 16 else 4
                par = ti % 2
                vb = (4, 5) if par == 0 else (0, 1)
                mb = (6, 7) if par == 0 else (2, 3)
                vnk = [("TB", par, b_) for b_ in range(5)]
                return t0, m, bi, par, vb, mb, vnk, VST[0:m, par:par + 1], ("VST", par)

            def e1_a(ti):
                t0, m, bi, par, vb, mb, vnk, vst, vstk = e1_cfg(ti)
                fns = []
                for q in range(4):
                    for kc in range(8):
                        fns.append(I("matmul", PSA[0:m, vb[q // 2], (q % 2) * 256:(q % 2) * 256 + 256],
                                     lhsT=H[:, kc, t0:t0 + m], rhs=wvs[q][0][:, kc, :], start=(kc == 0), stop=(kc == 7)))
                P.group("pe", fns, reads=[k for _, k in wvs] + [("H", kc, bi) for kc in range(8)], writes=psk(*vb))
                vps = PSA[0:m, vb[0]:vb[0] + 2, :]
                vn = TB[0:m, par, 0:1024]
                P.op("act", I("activation", out=jk[0:m], in_=vps, func=AF.Square, accum_out=vst),
                     reads=psk(*vb), writes=[TK[0], vstk])
                P.op("act", I("activation", out=vst, in_=vst, func=AF.Sqrt, bias=EPSC[0:m, 0:1], scale=1.0 / D),
                     reads=[vstk, "EPSC"], writes=[vstk])
                P.op("dve", I("reciprocal", out=vst, in_=vst), reads=[vstk], writes=[vstk])
                P.op("dve", I("scalar_tensor_tensor", out=vn.rearrange("p (a b) -> p a b", a=2), in0=vps, scalar=vst,
                              in1=GVT[0:m, :].rearrange("p (a b) -> p a b", a=2), op0=ALU.mult, op1=ALU.mult),
                     reads=psk(*vb) + [vstk, TK[2]], writes=[vnk])
                if ti == 16:
                    P.op("dve", I("scalar_tensor_tensor", out=OST[0:16, :].rearrange("p (a b) -> p a b", a=2), in0=vps,
                                  scalar=vst, in1=GVT[0:16, :].rearrange("p (a b) -> p a b", a=2),
                                  op0=ALU.mult, op1=ALU.mult),
                         reads=psk(*vb) + [vstk, TK[2]], writes=["OST"])
                    out_toks.append(P.dma("sp", I("dma_start", out=nsv_d[j], in_=OST[0:16, :]), reads=["OST"]))

            def e1_b(ti):
                t0, m, bi, par, vb, mb, vnk, vst, vstk = e1_cfg(ti)
                vn = TB[0:m, par, 0:1024]
                if ti == 16:
                    fns = [I("matmul", PSA[:, mb[0], cc * 16:(cc + 1) * 16], lhsT=vn[:, cc * 128:(cc + 1) * 128],
                             rhs=DG[:, cc // 2, :], start=True, stop=True) for cc in range(8)]
                    P.group("pe", fns, reads=[vnk, "DG"], writes=psk(mb[0]))
                    for cc in range(8):
                        P.op("dve", I("tensor_scalar", out=YA[:, cc, 2048:2064], in0=PSA[:, mb[0], cc * 16:(cc + 1) * 16],
                                      scalar1=BS8[:, cc, 0:1], scalar2=None, op0=ALU.add),
                             reads=psk(mb[0]) + [TK[2]], writes=[("YA", cc, 4)])
                else:
                    fns = [I("matmul", PSA[:, mb[cc // 4], (cc % 4) * 128:(cc % 4) * 128 + 128],
                             lhsT=vn[:, cc * 128:(cc + 1) * 128], rhs=WT[:, cc // 2, :], start=True, stop=True)
                           for cc in range(8)]
                    P.group("pe", fns, reads=[vnk, "WT"], writes=psk(*mb))
                    P.op("dve", I("tensor_tensor", out=YA[:, :, t0:t0 + 128],
                                  in0=PSA[:, mb[0]:mb[0] + 2, :].rearrange("p a (c t) -> p (a c) t", c=4),
                                  in1=BS8, op=ALU.add),
                         reads=psk(*mb) + [TK[2]], writes=[("YA", cc, bi) for cc in range(8)])

            e1_a(0)
            for ti in range(1, 17):
                e1_a(ti)
                e1_b(ti - 1)
            e1_b(16)

            for pr in range(4):
                wg, wgk = wload(wcols(wab, 0, 8, 4 * D + pr * 256, 256))
                wu, wuk = wload(wcols(wab, 0, 8, 2 * D + pr * 256, 256))
                for jj in range(2):
                    def ev_gb(bi, b, n, jj=jj):
                        t0 = BLK[bi][0]
                        P.op("act", I("activation", out=TB[:, jj, t0:t0 + n], in_=PSA[:, b, 0:n], func=AF.Silu),
                             reads=psk(b), writes=[("TB", jj, bi)])
                    proj(wg, wgk, jj * 128, 8, H, "H", ev_gb)
                for jj in range(2):
                    cc = pr * 2 + jj

                    def ev_u(bi, b, n, jj=jj, cc=cc):
                        t0 = BLK[bi][0]
                        P.op("dve", I("tensor_tensor", out=YA[:, cc, t0:t0 + n], in0=PSA[:, b, 0:n],
                                                              in1=YA[:, cc, t0:t0 + n], op=ALU.mult),
                             reads=psk(b) + [("YA", cc, bi)], writes=[("YA", cc, bi)])
                        P.op("dve", I("tensor_tensor", out=YA[:, cc, t0:t0 + n], in0=YA[:, cc, t0:t0 + n],
                                                              in1=TB[:, jj, t0:t0 + n], op=ALU.mult),
                             reads=[("YA", cc, bi), ("TB", jj, bi)], writes=[("YA", cc, bi)])
                    proj(wu, wuk, jj * 128, 8, H, "H", ev_u)
            out_proj(w_out_ab_d[j], D, YA, "YA")

            for k in range(3):
                P.op("dve", I("memset", TT[:, k, 0:16], 0.0), writes=[TK[k]])
            for g in range(4):
                w = POOL_W[g]
                wx, wxk = wload(wcols(wab, 0, 8, g * 256, 256))
                wga, wgak = wload(wcols(wab, 0, 8, D + g * 256, 256))
                pm, pmk = wload(pmaps_d[j, g].rearrange("(kc p) d -> p kc d", p=128))
                for jj in range(2):
                    cc = g * 2 + jj

                    def ev_xa(bi, b, n, cc=cc):
                        t0 = BLK[bi][0]
                        if bi == 4:
                            P.op("act", I("activation", out=XAS[:, cc, :], in_=PSA[:, b, 0:n], func=AF.Copy),
                                 reads=psk(b), writes=["XAS"])
                        else:
                            P.op("act", I("activation", out=T0[:, 16 + t0:16 + t0 + n], in_=PSA[:, b, 0:n],
                                                               func=AF.Copy),
                                 reads=psk(b), writes=[TK[0]])
                            if bi == 3:
                                P.op("act", I("activation", out=XAL[:, cc, :], in_=PSA[:, b, 480:512],
                                                                   func=AF.Copy),
                                     reads=psk(b), writes=["XAL"])
                    proj(wx, wxk, jj * 128, 8, H, "H", ev_xa)
                    bufs = [T0, T1, T2]
                    keys = [TK[0], TK[1], TK[2]]
                    cur = 0
                    sh = 1
                    nxt = 1
                    while sh < w:
                        P.op("dve", I("tensor_tensor",
                            out=bufs[nxt][:, 16:2064], in0=bufs[cur][:, 16:2064], in1=bufs[cur][:, 16 - sh:2064 - sh],
                            op=ALU.add), reads=[keys[cur]], writes=[keys[nxt]])
                        cur = nxt
                        nxt = 2 if cur == 1 else 1
                        sh *= 2
                    S, Sk = bufs[cur], keys[cur]
                    P.op("dve", I("scalar_tensor_tensor",
                        out=TB[:, jj, 0:2048], in0=S[:, 16:2064], scalar=1.0 / w, in1=T0[:, 16:2064],
                        op0=ALU.mult, op1=ALU.subtract), reads=[Sk, TK[0]], writes=[("TB", jj, bi) for bi in range(4)])
                    P.op("dve", I("tensor_tensor", out=SM1[:, 0:16], in0=S[:, 16:32], in1=INVT[:, g, :],
                                                                  op=ALU.mult), reads=[Sk, "INVT"], writes=["SM1"])
                    P.op("dve", I("tensor_tensor", out=TB[:, jj, 0:16], in0=SM1[:, 0:16], in1=T0[:, 16:32],
                                                                 op=ALU.subtract),
                         reads=["SM1", TK[0]], writes=[("TB", jj, 0)])
                    P.op("dve", I("tensor_tensor", out=SM2[:, 0:16], in0=CSUM[:, cc, :], in1=XAS[:, cc, :],
                                                                 op=ALU.add), reads=["CSUM", "XAS"], writes=["SM2"])
                    P.op("dve", I("scalar_tensor_tensor",
                        out=TB[:, jj, 2048:2064], in0=SM2[:, 0:16], scalar=1.0 / w, in1=XAS[:, cc, :],
                        op0=ALU.mult, op1=ALU.subtract), reads=["SM2", "XAS"], writes=[("TB", jj, 4)])
                for jj in range(2):
                    cc = g * 2 + jj

                    def ev_ga(bi, b, n, cc=cc):
                        t0 = BLK[bi][0]
                        P.op("act", I("activation", out=YA[:, cc, t0:t0 + n], in_=PSA[:, b, 0:n], func=AF.Silu),
                             reads=psk(b), writes=[("YA", cc, bi)])
                    proj(wga, wgak, jj * 128, 8, H, "H", ev_ga)
                for jj in range(2):
                    cc = g * 2 + jj

                    def ev_ya(bi, b, n, cc=cc):
                        t0 = BLK[bi][0]
                        P.op("dve", I("scalar_tensor_tensor",
                            out=YA[:, cc, t0:t0 + n], in0=PSA[:, b, 0:n], scalar=gcol(96 + j * 8 + cc),
                            in1=YA[:, cc, t0:t0 + n], op0=ALU.mult, op1=ALU.mult),
                            reads=psk(b) + [("YA", cc, bi), "GV"], writes=[("YA", cc, bi)])
                    proj(pm, pmk, jj * 128, 2, TB, "TB", ev_ya)
            P.group("pe", [I("transpose", out=PSA[0:32, 4 + cc // 4, (cc % 4) * 128:(cc % 4) * 128 + 128],
                                                        in_=XAL[:, cc, :], identity=ident[:]) for cc in range(8)],
                    reads=["XAL", "ident"], writes=psk(4, 5))
            P.op("dve", I("tensor_copy", out=OST[:, :].rearrange("p (a b) -> p a b", a=2), in_=PSA[0:32, 4:6, :]),
                 reads=psk(4, 5), writes=["OST"])
            out_toks.append(P.dma("sp", I("dma_start", out=npp_d[j], in_=OST[17:32, :]), reads=["OST"]))
            P.group("pe", [I("transpose", out=PSA[0:16, 4 + cc // 4, (cc % 4) * 128:(cc % 4) * 128 + 128],
                                                        in_=XAS[:, cc, :], identity=ident[:]) for cc in range(8)],
                    reads=["XAS", "ident"], writes=psk(4, 5))
            P.op("dve", I("tensor_copy", out=OST[0:16, :].rearrange("p (a b) -> p a b", a=2), in_=PSA[0:16, 4:6, :]),
                 reads=psk(4, 5), writes=["OST"])
            out_toks.append(P.dma("sp", I("dma_start", out=nps_d[j][:, 14, :], in_=OST[0:16, :]), reads=["OST"]))
            out_proj(w_out_ab_d[j], 0, YA, "YA")

        def dummy_h():
            return None

        def odd_layer(i):
            j = i // 2
            wc = w_in_c_d[j]
            sct = T0[0:32, 0:2048]
            P.dma("sp", I("dma_start", out=sct, in_=sconv_d[j].rearrange("s r c -> (s r) c")), writes=[TK[0]])
            out_toks.append(P.dma("sp", I("dma_start", out=ncs_d[j][:, 0, :], in_=sconv_d[j][:, 1, :])))
            for q in range(4):
                P.group("pe", [I("transpose", out=PSA[:, 4, k * 32:(k + 1) * 32],
                                                              in_=sct[:, (q * 4 + k) * 128:(q * 4 + k + 1) * 128],
                                                              identity=ident[0:32, 0:32]) for k in range(4)],
                        reads=[TK[0], "ident"], writes=psk(4))
                P.op("dve", I("tensor_copy", out=SCX[:, q * 4:(q + 1) * 4, :],
                                                         in_=PSA[:, 4, 0:128].rearrange("p (k x) -> p k x", k=4)),
                     reads=psk(4), writes=["SCX"])
            rms_norm_fm(i * 8, lambda kc, t0, n: H[:, kc, t0:t0 + n], "H")
            P.op("dve", I("memset", T1[:, 0:16], 0.0), writes=[TK[1]])
            for half in range(2):
                for pr in range(4):
                    c0 = half * D + pr * 256
                    wcg, wcgk = wload(wcols(wc, 0, 8, 2 * D + c0, 256))
                    wxc, wxck = wload(wcols(wc, 0, 8, 4 * D + c0, 256))
                    wbg, wbgk = wload(wcols(wc, 0, 8, c0, 256))
                    wgg, wggk = wload(wcols(wc, 0, 8, 6 * D + c0, 256))
                    for jj in range(2):
                        fc = half * 8 + pr * 2 + jj
                        yc = fc % 8
                        cw = 120 + j * 48 + fc

                        def ev_cg(bi, b, n):
                            t0 = BLK[bi][0]
                            P.op("act", I("activation", out=T0[:, 16 + t0:16 + t0 + n], in_=PSA[:, b, 0:n],
                                                               func=AF.Copy), reads=psk(b), writes=[TK[0]])
                        proj(wcg, wcgk, jj * 128, 8, H, "H", ev_cg)

                        def ev_xc(bi, b, n, fc=fc):
                            t0 = BLK[bi][0]
                            P.op("dve", I("tensor_tensor", out=T1[:, 16 + t0:16 + t0 + n], in0=PSA[:, b, 0:n],
                                                                  in1=T0[:, 16 + t0:16 + t0 + n], op=ALU.mult),
                                 reads=psk(b) + [TK[0]], writes=[TK[1]])
                        proj(wxc, wxck, jj * 128, 8, H, "H", ev_xc)
                        P.op("act", I("activation", out=CXL[:, :, fc], in_=T1[:, 2062:2064], func=AF.Copy),
                             reads=[TK[1]], writes=["CXL"])
                        P.op("act", I("activation", out=CXS[:, fc, :], in_=T1[:, 2064:2080], func=AF.Copy),
                             reads=[TK[1]], writes=["CXS"])
                        P.op("act", I("activation", out=T2[:, 16:2064], in_=T1[:, 14:2062], func=AF.Copy,
                                                                  scale=gcol(cw)), reads=[TK[1], "GV"], writes=[TK[2]])
                        P.op("dve", I("scalar_tensor_tensor",
                            out=T2[:, 16:2064], in0=T1[:, 15:2063], scalar=gcol(cw + 16), in1=T2[:, 16:2064],
                            op0=ALU.mult, op1=ALU.add), reads=[TK[1], TK[2], "GV"], writes=[TK[2]])
                        P.op("dve", I("scalar_tensor_tensor",
                            out=T2[:, 16:2064], in0=T1[:, 16:2064], scalar=gcol(cw + 32), in1=T2[:, 16:2064],
                            op0=ALU.mult, op1=ALU.add), reads=[TK[1], TK[2], "GV"], writes=[TK[2]])
                        scx = SCX[:, fc, :].rearrange("p (s r) -> p s r", r=2)
                        P.op("dve", I("tensor_scalar",
                            out=T2[:, 2064:2080], in0=scx[:, :, 0], scalar1=gcol(cw), scalar2=None, op0=ALU.mult),
                            reads=["SCX", "GV"], writes=[TK[2]])
                        P.op("dve", I("scalar_tensor_tensor",
                            out=T2[:, 2064:2080], in0=scx[:, :, 1], scalar=gcol(cw + 16), in1=T2[:, 2064:2080],
                            op0=ALU.mult, op1=ALU.add), reads=["SCX", TK[2], "GV"], writes=[TK[2]])
                        P.op("dve", I("scalar_tensor_tensor",
                            out=T2[:, 2064:2080], in0=T1[:, 2064:2080], scalar=gcol(cw + 32), in1=T2[:, 2064:2080],
                            op0=ALU.mult, op1=ALU.add), reads=[TK[1], TK[2], "GV"], writes=[TK[2]])

                        def ev_bg(bi, b, n):
                            t0 = BLK[bi][0]
                            P.op("dve", I("tensor_tensor", out=T0[:, 16 + t0:16 + t0 + n], in0=PSA[:, b, 0:n],
                                                                  in1=T2[:, 16 + t0:16 + t0 + n], op=ALU.mult),
                                 reads=psk(b) + [TK[2]], writes=[TK[0]])
                        proj(wbg, wbgk, jj * 128, 8, H, "H", ev_bg)

                        def ev_g(bi, b, n, yc=yc):
                            t0 = BLK[bi][0]
                            P.op("act", I("activation", out=YA[:, yc, t0:t0 + n], in_=PSA[:, b, 0:n], func=AF.Silu),
                                 reads=psk(b), writes=[("YA", yc, bi)])
                            P.op("dve", I("tensor_tensor", out=YA[:, yc, t0:t0 + n], in0=YA[:, yc, t0:t0 + n],
                                                                  in1=T0[:, 16 + t0:16 + t0 + n], op=ALU.mult),
                                 reads=[("YA", yc, bi), TK[0]], writes=[("YA", yc, bi)])
                        proj(wgg, wggk, jj * 128, 8, H, "H", ev_g)
                out_proj(w_out_c_d[j], half * D, YA, "YA")
            P.op("pe", I("transpose", out=PSA[0:32, 4, 0:128], in_=CXL[:].rearrange("p r f -> p (r f)"),
                                             identity=ident[:]), reads=["CXL", "ident"], writes=psk(4))
            P.op("dve", I("tensor_copy", out=OST[0:32, 0:128], in_=PSA[0:32, 4, 0:128]), reads=psk(4), writes=["OST"])
            out_toks.append(P.dma("sp", I("dma_start",
                out=ncp_d[j].rearrange("r (fc c) -> (r fc) c", c=128), in_=OST[0:32, 0:128]), reads=["OST"]))
            for q in range(4):
                P.group("pe", [I("transpose", out=PSA[0:16, 4 + q, k * 128:(k + 1) * 128],
                                                              in_=CXS[:, q * 4 + k, :], identity=ident[:])
                               for k in range(4)], reads=["CXS", "ident"], writes=psk(4 + q))
            cst = T0[0:16, 0:2048]
            P.op("dve", I("tensor_copy", out=cst.rearrange("p (a b) -> p a b", a=4), in_=PSA[0:16, 4:8, :]),
                 reads=psk(4, 5, 6, 7), writes=[TK[0]])
            out_toks.append(P.dma("sp", I("dma_start", out=ncs_d[j][:, 1, :], in_=cst), reads=[TK[0]]))

        def att_mem_pre(i):
            mnt = TBf[:, 0:2048].rearrange("p (k m) -> p k m", k=8)
            memst = T2[:, 0:2048].rearrange("p (a b) -> p a b", a=2)
            P.dma("sp", I("dma_start", out=memst, in_=mem_d.rearrange("(a p) f -> p a f", p=128)), writes=[TK[2]])
            for a in range(2):
                P.op("act", I("activation", out=TB[:, 1, 0:1024], in_=memst[:, a, :], func=AF.Square,
                                                        accum_out=VST[:, a:a + 1]),
                     reads=[TK[2]], writes=[TBALL, [("VST", 0), ("VST", 1)]])
            P.op("act", I("activation", out=VST[:], in_=VST[:], func=AF.Sqrt, bias=EPSC[:, 0:1], scale=1.0 / D),
                 reads=[[("VST", 0), ("VST", 1)], "EPSC"], writes=[[("VST", 0), ("VST", 1)]])
            P.op("dve", I("reciprocal", out=VST[:], in_=VST[:]), reads=[[("VST", 0), ("VST", 1)]], writes=[[("VST", 0), ("VST", 1)]])
            for a in range(2):
                P.op("dve", I("tensor_scalar", out=memst[:, a, :], in0=memst[:, a, :], scalar1=VST[:, a:a + 1],
                                                           scalar2=None, op0=ALU.mult),
                     reads=[TK[2], [("VST", 0), ("VST", 1)]], writes=[TK[2]])

        def att_mem_pe(i):
            mnt = TBf[:, 0:2048].rearrange("p (k m) -> p k m", k=8)
            memst = T2[:, 0:2048].rearrange("p (a b) -> p a b", a=2)
            for kc in range(8):
                b = next_acc()
                P.group("pe", [I("transpose", out=PSA[:, b, a * 128:(a + 1) * 128],
                                                                      in_=memst[:, a, kc * 128:(kc + 1) * 128],
                                                                      identity=ident[:]) for a in range(2)],
                        reads=[TK[2], "ident"], writes=psk(b))
                P.op("dve", I("tensor_scalar", out=mnt[:, kc, :], in0=PSA[:, b, 0:256],
                                                                  scalar1=gcol(64 + i * 8 + kc), scalar2=None, op0=ALU.mult),
                     reads=psk(b) + ["GV"], writes=[TBALL])
            for which, (wd, od) in enumerate([(wk_d, nmk_d), (wv_d, nmv_d)]):
                for q in range(4):
                    wv_, wk_ = wload(wcols(wd[i], 0, 8, q * 256, 256))
                    for a in range(2):
                        b = next_acc()
                        P.group("pe", [I("matmul",
                            PSA[:, b, 0:256], lhsT=mnt[:, kc, a * 128:(a + 1) * 128], rhs=wv_[:, kc, :],
                            start=(kc == 0), stop=(kc == 7)) for kc in range(8)],
                            reads=[TBALL, wk_], writes=psk(b))
                        stg = T0[:, (a * 4 + q) * 256:(a * 4 + q + 1) * 256]
                        P.op("act", I("activation", out=stg, in_=PSA[:, b, 0:256], func=AF.Copy),
                             reads=psk(b), writes=[TK[0]])
                        if which == 1:
                            P.op("dve", I("tensor_copy", out=VM[:, a, q * 256:(q + 1) * 256],
                                                                              in_=PSA[:, b, 0:256]),
                                 reads=psk(b), writes=["VM"])
                    if which == 0:
                        for jj in range(2):
                            b = next_acc()
                            dc = q * 2 + jj
                            P.group("pe", [I("matmul",
                                PSA[:, b, 0:256], lhsT=wv_[:, kc, jj * 128:(jj + 1) * 128], rhs=mnt[:, kc, :],
                                start=(kc == 0), stop=(kc == 7)) for kc in range(8)],
                                reads=[TBALL, wk_], writes=psk(b))
                            P.op("act", I("activation", out=KT[:, dc, :], in_=PSA[:, b, 0:256],
                                                                          func=AF.Copy), reads=psk(b), writes=["KT"])
                for a in range(2):
                    out_toks.append(P.dma("sp", I("dma_start", out=od[i][a * 128:(a + 1) * 128, :],
                                                  in_=T0[:, a * 1024:(a + 1) * 1024]), reads=[TK[0]]))


        def attention(i):
            P.mute = not DBG.get("a_q", True)
            rms_norm_fm(32 + i * 8, lambda kc, t0, n: H[:, kc, t0:t0 + n], "H")

            for pr in range(4):
                wq, wqk = wload(wcols(wq_d[i], 0, 8, pr * 256, 256))
                for jj in range(2):
                    dc = pr * 2 + jj

                    def ev_q(bi, b, n, dc=dc):
                        t0 = BLK[bi][0]
                        P.op("act", I("activation", out=YA[:, dc, t0:t0 + n], in_=PSA[:, b, 0:n], func=AF.Copy,
                                                           scale=0.0625), reads=psk(b), writes=[("YA", dc, bi)])
                    proj(wq, wqk, jj * 128, 8, H, "H", ev_q, blks=range(4))
                b = next_acc()
                P.group("pe", [I("matmul", PSA[0:16, b, 0:256], lhsT=H[:, kc, 2048:2064],
                                                                     rhs=wq[:, kc, :], start=(kc == 0), stop=(kc == 7))
                               for kc in range(8)], reads=[wqk] + [("H", kc, 4) for kc in range(8)], writes=psk(b))
                P.op("act", I("activation", out=QS[:, pr * 256:(pr + 1) * 256], in_=PSA[0:16, b, 0:256],
                                                               func=AF.Copy, scale=0.0625), reads=psk(b), writes=["OST"])

            P.mute = not DBG.get("a_core", True)
            kvb = [[TT[:, a_, :].bitcast(BF16)[:, k * 2048:(k + 1) * 2048].rearrange("p (a c) -> p a c", a=2)
                    for k in range(2)] for a_ in range(2)]
            ex = TB[:, 1, 0:1024].rearrange("p (a b) -> p a b", a=2)
            EXK = [("TB", 1, 0), ("TB", 1, 1)]

            def sample_pv(s):
                par = s % 2
                vsb = kvb[par][1]
                for dc in range(8):
                    P.group("pe", [I("matmul", PSA[:, 5, dc * 16 + s:dc * 16 + s + 1],
                                     lhsT=vsb[:, mt, dc * 128:(dc + 1) * 128],
                                     rhs=ES[:, s, mt, dc // 2:dc // 2 + 1], start=(mt == 0), stop=(mt == 1))
                                   for mt in range(2)],
                            reads=[TK[par][1], ("ES", s)], writes=psk(5))

            n_it = 0
            for bi in range(4):
                t0 = BLK[bi][0]
                for h in range(4):
                    s = n_it
                    par = s % 2
                    n_it += 1
                    ksb, vsb = kvb[par][0], kvb[par][1]
                    P.dma("pool", I("dma_start", out=ksb, in_=ck_d[i, s].rearrange("(a p) c -> p a c", p=128)),
                          writes=[TK[par][0]])
                    P.dma("pool", I("dma_start", out=vsb, in_=cv_d[i, s].rearrange("(a p) c -> p a c", p=128)),
                          writes=[TK[par][1]])
                    for mt in range(2):
                        P.group("pe", [I("matmul", PSA[:, mt, :], lhsT=KT[:, 2 * h + dcc, mt * 128:(mt + 1) * 128],
                                         rhs=YA[:, 2 * h + dcc, t0:t0 + 512], start=(dcc == 0), stop=(dcc == 1))
                                       for dcc in range(2)],
                                reads=["KT", ("YA", 2 * h, bi), ("YA", 2 * h + 1, bi)], writes=psk(mt))
                    for hf in range(2):
                        P.op("pe", I("matmul", PSA[:, 2 + hf, :], lhsT=SEL[:, s, :], rhs=QS[:, hf * 512:(hf + 1) * 512],
                                     start=True, stop=True), reads=["SEL", "OST"], writes=psk(2 + hf))
                    if s > 0:
                        sample_pv(s - 1)
                    P.op("act", I("activation", out=ex, in_=PSA[:, 0:2, :], func=AF.Exp), reads=psk(0, 1), writes=[EXK])
                    for hf in range(2):
                        for mt in range(2):
                            for hh in range(2):
                                hd = hf * 2 + hh
                                P.op("dve", I("scalar_tensor_tensor", out=TB[:, 0, 0:256],
                                              in0=ksb[:, mt, hd * 256:(hd + 1) * 256], scalar=1.0,
                                              in1=PSA[:, 2 + hf, hh * 256:hh * 256 + 256], op0=ALU.mult, op1=ALU.mult,
                                              accum_out=SC[:, s, mt, hd:hd + 1]),
                                     reads=[TK[par][0]] + psk(2 + hf), writes=[("TB", 0, 0), ("SC", s)])
                    P.group("pe", [I("matmul", PSA[:, 4, :], lhsT=onesb[:], rhs=ex[:, mt, :],
                                     start=(mt == 0), stop=(mt == 1)) for mt in range(2)],
                            reads=[EXK, "onesb"], writes=psk(4))
                    for dcc in range(2):
                        P.group("pe", [I("matmul", PSA[:, 6 + dcc, :], lhsT=VM[:, mt, (2 * h + dcc) * 128:(2 * h + dcc + 1) * 128],
                                         rhs=ex[:, mt, :], start=(mt == 0), stop=(mt == 1)) for mt in range(2)],
                                reads=[EXK, "VM"], writes=psk(6 + dcc))
                    P.op("act", I("activation", out=RS[:], in_=PSA[:, 4, :], func=AF.Ln), reads=psk(4), writes=[RSK])
                    P.op("act", I("activation", out=RS[:], in_=RS[:], func=AF.Exp, scale=-1.0), reads=[RSK], writes=[RSK])
                    for dcc in range(2):
                        P.op("dve", I("tensor_tensor", out=YA[:, 2 * h + dcc, t0:t0 + 512],
                                                                           in0=PSA[:, 6 + dcc, :], in1=RS[:], op=ALU.mult),
                             reads=psk(6 + dcc) + [RSK], writes=[("YA", 2 * h + dcc, bi)])
                    P.op("act", I("activation", out=ES[:, s], in_=SC[:, s], func=AF.Exp),
                         reads=[("SC", s)], writes=[("ES", s)])
            sample_pv(NS - 1)
            P.mute = not DBG.get("a_samp", True)
            P.op("pe", I("matmul", PSA[:, 4, 0:128], lhsT=onesb[:], rhs=ES[:].rearrange("p s m h -> p (s m h)"),
                                          start=True, stop=True), reads=[("ES", s) for s in range(NS)] + ["onesb"],
                 writes=psk(4))
            dv = PSA[:, 4, 0:128].rearrange("p (s m h) -> p s m h", s=16, m=2)
            P.op("dve", I("tensor_copy", out=DS[:], in_=dv[:, :, 0, :]), reads=psk(4), writes=["DS"])
            P.op("dve", I("tensor_tensor", out=DS[:], in0=dv[:, :, 1, :], in1=DS[:], op=ALU.add),
                 reads=psk(4) + ["DS"], writes=["DS"])
            P.op("dve", I("reciprocal", out=DS[:], in_=DS[:]), reads=["DS"], writes=["DS"])
            for dc in range(8):
                P.op("dve", I("tensor_tensor", out=YA[:, dc, 2048:2064], in0=PSA[:, 5, dc * 16:(dc + 1) * 16],
                                                             in1=DS[:, :, dc // 2], op=ALU.mult),
                     reads=psk(5) + ["DS"], writes=[("YA", dc, 4)])
            P.mute = not DBG.get("a_out", True)
            hooks = None
            if i + 1 < DBG["layers"]:
                hooks = {0: (lambda: att_mem_pre(i + 1)), 2: (lambda: att_mem_pe(i + 1))}
            out_proj(wo_d[i], 0, YA, "YA", hooks)
            P.mute = False

        if DBG["att"] and DBG["layers"] > 0:
            att_mem_pre(0)
            att_mem_pe(0)
        for i in range(DBG["layers"]):
            if DBG["mix"]:
                if i % 2 == 0:
                    even_layer(i)
                else:
                    odd_layer(i)
            if DBG["att"]:
                attention(i)

        fsubs = []
        kk = 0
        for bi in range(5):
            t0b, nb = BLK[bi]
            for t0 in range(t0b, t0b + nb, 256):
                fsubs.append((bi, t0, min(256, t0b + nb - t0), kk % 2))
                kk += 1
        otl = {"k": 0}

        def fin_a(bi, t0, n, p):
            sq = TBf[:, p * 2048:p * 2048 + 8 * n].rearrange("p (k n) -> p k n", k=8)
            P.op("act", I("activation", out=sq, in_=XP[:, :, t0:t0 + n], func=AF.Square),
                 reads=[("XP", kc, bi) for kc in range(8)], writes=[SQK[p]])
            P.group("pe", [I("matmul", PSA[:, 4 + p, 0:n], lhsT=onesb[:], rhs=sq[:, kc, :],
                             start=(kc == 0), stop=(kc == 7)) for kc in range(8)],
                    reads=[SQK[p], "onesb"], writes=psk(4 + p))

        def fin_b(bi, t0, n, p):
            rs = RS[:, p * 256:p * 256 + n]
            yst = TT[:, p, 0:2048].rearrange("p (k n) -> p k n", k=8)
            P.op("act", I("activation", out=rs, in_=PSA[:, 4 + p, 0:n], func=AF.Ln, bias=EPSC[:, 0:1], scale=1.0 / D),
                 reads=psk(4 + p) + ["EPSC"], writes=[RSK[p]])
            P.op("act", I("activation", out=rs, in_=rs, func=AF.Exp, scale=-0.5), reads=[RSK[p]], writes=[RSK[p]])
            for kc in range(8):
                P.op("dve", I("scalar_tensor_tensor", out=yst[:, kc, 0:n], in0=XP[:, kc, t0:t0 + n],
                              scalar=gcol(112 + kc), in1=rs, op0=ALU.mult, op1=ALU.mult),
                     reads=[("XP", kc, bi), RSK[p], "GV"], writes=[TK[p]])

        def fin_c(bi, t0, n, p):
            yst = TT[:, p, 0:2048].rearrange("p (k n) -> p k n", k=8)
            m = 128 if bi < 4 else NS
            for tl in range((n + 127) // 128):
                o = otl["k"] % 2
                otl["k"] += 1
                for hf in range(2):
                    b = (0, 1, 2, 3, 6, 7)[(otl["k"] * 2 + hf) % 6]
                    P.group("pe", [I("transpose", out=PSA[0:m, b, k * 128:(k + 1) * 128],
                                     in_=yst[:, hf * 4 + k, tl * 128:tl * 128 + m], identity=ident[:])
                                   for k in range(4)], reads=[TK[p], "ident"], writes=psk(b))
                    o0 = o * 1024 + hf * 512
                    if hf:
                        P.op("act", I("activation", out=T2[0:m, o0:o0 + 512], in_=PSA[0:m, b, :], func=AF.Copy),
                             reads=psk(b), writes=[TK[2][o]])
                    else:
                        P.op("dve", I("tensor_copy", out=T2[0:m, o0:o0 + 512], in_=PSA[0:m, b, :]),
                             reads=psk(b), writes=[TK[2][o]])
                if bi < 4:
                    r0 = t0 + tl * 128
                    out_toks.append(P.dma("sp", I("dma_start", out=yp_d[r0:r0 + 128, :], in_=T2[:, o * 1024:o * 1024 + 1024]),
                                          reads=[TK[2][o]]))
                else:
                    out_toks.append(P.dma("sp", I("dma_start", out=ys_d, in_=T2[0:16, o * 1024:o * 1024 + 1024]),
                                          reads=[TK[2][o]]))

        fin_a(*fsubs[0])
        for k in range(1, len(fsubs)):
            fin_a(*fsubs[k])
            fin_b(*fsubs[k - 1])
            fin_c(*fsubs[k - 1])
        fin_b(*fsubs[-1])
        fin_c(*fsubs[-1])

        P.wait("sp", out_toks)
        P.replay(st)
    return nc


_CACHE = {}

W_NAMES = ["norm_mix_g", "norm_xattn_g", "norm_mem_g", "w_in_ab", "pool_maps", "pool_scale", "sgu_w", "sgu_b",
           "sgu_g", "w_out_ab", "w_in_c", "conv_w", "w_out_c", "w_q", "w_k", "w_v", "w_o", "norm_final_g"]


def kernel(**inputs):
    f = lambda a: np.ascontiguousarray(np.asarray(a, dtype=np.float32))
    if "nc" not in _CACHE:
        _CACHE["nc"] = build_program()
    nc = _CACHE["nc"]
    shared = {k: f(inputs[k]) for k in W_NAMES}
    xp = np.asarray(inputs["x_prompt"]); xs = np.asarray(inputs["x_sample"])
    mem = np.asarray(inputs["mem_prompt"]); sp = np.asarray(inputs["state_pool"])
    scv = np.asarray(inputs["state_conv"]); ck = np.asarray(inputs["cache_mem_k"]); cv = np.asarray(inputs["cache_mem_v"])
    in_maps = []
    for c in range(NCORES):
        sl = slice(c * NS, (c + 1) * NS)
        m = dict(shared)
        m["xp"] = f(xp[c]); m["xs"] = f(xs[sl, 0]); m["mem"] = f(mem[c])
        m["spool"] = f(sp[:, sl]); m["sconv"] = f(scv[:, sl])
        m["ck"] = f(ck[:, sl].reshape(DEPTH, NS, 256, D)); m["cv"] = f(cv[:, sl].reshape(DEPTH, NS, 256, D))
        in_maps.append(m)
    res = run_bass_kernel_spmd(nc, in_maps, core_ids=list(range(NCORES)))
    R = res.results
    cat = lambda k, ax: np.concatenate([np.expand_dims(r[k], ax) if False else r[k] for r in R], axis=ax)
    y_prompt = np.stack([r["yp"] for r in R], 0)
    y_sample = np.concatenate([r["ys"] for r in R], 0)[:, None, :]
    npp = np.stack([r["npp"] for r in R], 1)
    nps = np.concatenate([r["nps"] for r in R], 1)
    ncp = np.stack([r["ncp"] for r in R], 1)
    ncs = np.concatenate([r["ncs"] for r in R], 1)
    nsv = np.concatenate([r["nsv"] for r in R], 1)[:, :, None, :]
    nmk = np.stack([r["nmk"] for r in R], 1).reshape(DEPTH, NCORES, 256, 4, 256)
    nmv = np.stack([r["nmv"] for r in R], 1).reshape(DEPTH, NCORES, 256, 4, 256)
    return (y_prompt, y_sample, npp, nps, ncp, ncs, nsv, nmk, nmv)
```

```python
from contextlib import ExitStack
import numpy as np
import concourse.bass as bass
import concourse.mybir as mybir
from concourse.bass_utils import run_bass_kernel_spmd

F32 = mybir.dt.float32
BF16 = mybir.dt.bfloat16
I32 = mybir.dt.int32
AF = mybir.ActivationFunctionType
ALU = mybir.AluOpType
AX = mybir.AxisListType

ENGS = ("pe", "act", "dve", "pool", "sp")
NCORES = 8
D = 1024
SEQ = 2048
NS = 16
NT = SEQ + NS
DEPTH = 4
EPS = 1e-6
BLK = [(0, 512), (512, 512), (1024, 512), (1536, 512), (2048, NS)]
POOL_W = (2, 4, 8, 16)
DBG = {"layers": DEPTH, "mix": True, "att": True}


class Prog:
    def __init__(self, nc, n_dma_sems=40):
        self.nc = nc
        self.ops = {e: [] for e in ENGS}
        self.cnt = {e: 0 for e in ENGS}
        self.waited = {e: {} for e in ENGS}
        self.last_w = {}
        self.readers = {}
        self.n_dma_sems = n_dma_sems
        self.dma_rr = {e: 0 for e in ENGS}
        self.dma_val = [0] * n_dma_sems
        self.dma_last_tok = [None] * n_dma_sems

    def _need(self, eng, toks):
        best = {}
        for t in toks:
            if t is None:
                continue
            k, v, _ = t
            if v > best.get(k, 0):
                best[k] = v
        out = []
        for k, v in best.items():
            if self.waited[eng].get(k, 0) >= v:
                continue
            self.waited[eng][k] = v
            out.append((k, v))
        return out

    @staticmethod
    def _flat(keys):
        out = []
        for k in keys:
            if isinstance(k, list):
                out.extend(Prog._flat(k))
            else:
                out.append(k)
        return out

    @staticmethod
    def _excl(reads, writes):
        reads = Prog._flat(reads); writes = Prog._flat(writes)
        ps = [k for k in reads if isinstance(k, tuple) and k[0] == "PS"]
        if ps:
            reads = [k for k in reads if not (isinstance(k, tuple) and k[0] == "PS")]
            writes = writes + [k for k in ps if k not in writes]
        return reads, writes

    def _deps(self, eng, reads, writes, extra):
        reads, writes = self._excl(reads, writes)
        toks = list(extra)
        for r in reads:
            toks.append(self.last_w.get(r))
        for w in writes:
            toks.append(self.last_w.get(w))
            for t in self.readers.get(w, ()):
                toks.append(t)
        return self._need(eng, toks)

    def _reg(self, tok, reads, writes):
        reads, writes = self._excl(reads, writes)
        for r in reads:
            self.readers.setdefault(r, []).append(tok)
        for w in writes:
            self.last_w[w] = tok
            self.readers[w] = []

    mute = False

    def group(self, eng, fns, reads=(), writes=(), extra=()):
        if self.mute:
            return None
        waits = self._deps(eng, reads, writes, extra)
        self.cnt[eng] += 1
        tok = (eng, self.cnt[eng], eng)
        n = len(fns)
        for i, fn in enumerate(fns):
            self.ops[eng].append((waits if i == 0 else (), fn, ("eng", eng) if i == n - 1 else None))
        self._reg(tok, reads, writes)
        return tok

    def op(self, eng, fn, reads=(), writes=(), extra=()):
        return self.group(eng, [fn], reads, writes, extra)

    def dma(self, q, fn, reads=(), writes=(), extra=()):
        if self.mute:
            return None
        lo, hi = (0, 24) if q != "pool" else (24, self.n_dma_sems)
        i = lo + self.dma_rr[q] % (hi - lo)
        self.dma_rr[q] += 1
        reads = self._flat(reads); writes = self._flat(writes)
        toks = list(extra)
        toks.append(self.dma_last_tok[i])
        for r in reads:
            toks.append(self.last_w.get(r))
        for w in writes:
            toks.append(self.last_w.get(w))
            toks.extend(self.readers.get(w, ()))
        waits = self._need(q, toks)
        self.dma_val[i] += 16
        tok = (("dma", i), self.dma_val[i], None)
        self.dma_last_tok[i] = tok
        self.ops[q].append((waits, fn, ("dma", i)))
        self._reg(tok, reads, writes)
        return tok

    def wait(self, eng, toks):
        waits = self._need(eng, toks)
        if waits:
            self.ops[eng].append((waits, None, None))

    def replay(self, stack):
        nc = self.nc
        semh = {}
        for e in ENGS:
            semh[e] = stack.enter_context(nc.semaphore("s_" + e))
        for i in range(self.n_dma_sems):
            semh[("dma", i)] = stack.enter_context(nc.semaphore("s_dma%d" % i))
        block = stack.enter_context(nc.Block())
        hmap = {"pe": block.tensor, "act": block.scalar, "dve": block.vector,
                "pool": block.gpsimd, "sp": block.sync}

        def mk(e):
            def body(eng):
                for waits, fn, sig in self.ops[e]:
                    for k, v in waits:
                        eng.wait_ge(semh[k], v)
                    if fn is None:
                        continue
                    ins = fn(eng)
                    if sig is not None:
                        if sig[0] == "eng":
                            ins.then_inc(semh[sig[1]], 1)
                        else:
                            ins.then_inc(semh[sig], 16)
            return body

        for e in ENGS:
            hmap[e](mk(e))


def I(method, *args, **kw):
    return lambda e: getattr(e, method)(*args, **kw)


def build_program():
    nc = bass.Bass("TRN2", target_bir_lowering=False)

    def din(name, shape):
        return nc.dram_tensor(name, list(shape), F32, kind="ExternalInput").ap()

    def dout(name, shape):
        return nc.dram_tensor(name, list(shape), F32, kind="ExternalOutput").ap()

    xp_d = din("xp", [SEQ, D]); xs_d = din("xs", [NS, D]); mem_d = din("mem", [256, D])
    spool_d = din("spool", [2, NS, 15, D]); sconv_d = din("sconv", [2, NS, 2, 2 * D])
    ck_d = din("ck", [DEPTH, NS, 256, D]); cv_d = din("cv", [DEPTH, NS, 256, D])
    g_mix_d = din("norm_mix_g", [DEPTH, D]); g_xa_d = din("norm_xattn_g", [DEPTH, D])
    g_mem_d = din("norm_mem_g", [DEPTH, D])
    w_in_ab_d = din("w_in_ab", [2, D, 5 * D]); pmaps_d = din("pool_maps", [2, 4, 256, 256])
    pscale_d = din("pool_scale", [2, D]); sgu_w_d = din("sgu_w", [2, 4, 128, 128])
    sgu_b_d = din("sgu_b", [2, 4, 128]); sgu_g_d = din("sgu_g", [2, D])
    w_out_ab_d = din("w_out_ab", [2, 2 * D, D]); w_in_c_d = din("w_in_c", [2, D, 8 * D])
    conv_w_d = din("conv_w", [2, 3, 2 * D]); w_out_c_d = din("w_out_c", [2, 2 * D, D])
    wq_d = din("w_q", [DEPTH, D, D]); wk_d = din("w_k", [DEPTH, D, D])
    wv_d = din("w_v", [DEPTH, D, D]); wo_d = din("w_o", [DEPTH, D, D])
    g_fin_d = din("norm_final_g", [D])

    yp_d = dout("yp", [SEQ, D]); ys_d = dout("ys", [NS, D])
    npp_d = dout("npp", [2, 15, D]); nps_d = dout("nps", [2, NS, 15, D])
    ncp_d = dout("ncp", [2, 2, 2 * D]); ncs_d = dout("ncs", [2, NS, 2, 2 * D])
    nsv_d = dout("nsv", [2, NS, D]); nmk_d = dout("nmk", [DEPTH, 256, D]); nmv_d = dout("nmv", [DEPTH, 256, D])

    P = Prog(nc)
    out_toks = []
    with ExitStack() as st:
        def sb(name, shape, dt):
            return st.enter_context(nc.sbuf_tensor(name, list(shape), dt))

        XP = sb("XP", [128, 8, NT], F32)
        H = sb("H", [128, 8, NT], BF16)
        YA = sb("YA", [128, 8, NT], BF16)
        NW = 4
        WB = sb("WB", [128, NW, 2048], BF16)
        TT = sb("TT", [128, 3, 2080], F32)
        TB = sb("TB", [128, 2, NT], BF16)
        KT = sb("KT", [128, 8, 256], BF16)
        VM = sb("VM", [128, 2, 1024], BF16)
        GV = sb("GV", [128, 216], F32)
        OST = sb("OST", [32, 1024], F32)
        RS = sb("RS", [128, 512], F32)
        ident = sb("ident", [128, 128], F32)
        onesf = sb("onesf", [128, 128], F32)
        onesb = sb("onesb", [128, 128], BF16)
        SEL = sb("SEL", [16, 16, 128], BF16)
        INVT = sb("INVT", [128, 4, 16], F32)
        IOT = sb("IOT", [128, 16], I32)
        WT = sb("WT", [128, 4, 128], BF16)
        WSC = sb("WSC", [16, 4], F32)
        DG = sb("DG", [16, 4, 16], BF16)
        XAS = sb("XAS", [128, 8, NS], F32)
        XAL = sb("XAL", [128, 8, 32], F32)
        CSUM = sb("CSUM", [128, 8, NS], F32)
        SM1 = sb("SM1", [128, 64], F32)
        SM2 = sb("SM2", [128, 64], F32)
        SCX = sb("SCX", [128, 16, 32], F32)
        CXL = sb("CXL", [128, 2, 16], F32)
        CXS = sb("CXS", [128, 16, NS], F32)
        SC = sb("SC", [128, 16, 2, 4], F32)
        ES = sb("ES", [128, 16, 2, 4], BF16)
        DS = sb("DS", [128, 16, 4], F32)
        VST = sb("VST", [128, 2], F32)
        PSA = st.enter_context(nc.psum_tensor("PSA", [128, 8, 512], F32))

        QS = OST[0:16, 0:512].bitcast(BF16)
        T0, T1, T2 = TT[:, 0, :], TT[:, 1, :], TT[:, 2, :]
        GVT = TT[:, 2, 0:1024]
        BS8 = TT[:, 2, 1024:2048].rearrange("p (c t) -> p c t", c=8)
        TBf = TB[:].rearrange("p a b -> p (a b)")
        TK = [["T0a", "T0b"], ["T1a", "T1b"], ["T2a", "T2b"]]
        TBALL = [("TB", r, b_) for r in range(2) for b_ in range(5)]

        def bank(b):
            return PSA[:, b, :]

        def psk(*bs):
            return [("PS", b) for b in bs]

        P.op("pool", I("memset", onesf[:], 1.0), writes=["onesf"])
        P.op("pool", I("memset", onesb[:], 1.0), writes=["onesb"])
        P.op("pool", I("affine_select", out=ident[:], in_=onesf[:], pattern=[[-1, 128]],
                                               compare_op=ALU.is_equal, fill=0.0, base=0,
                                               channel_multiplier=1),
             reads=["onesf"], writes=["ident"])
        P.op("pool", I("memset", SEL[:], 1.0), writes=["SEL"])
        P.op("pool", I("affine_select", out=SEL[:], in_=SEL[:], pattern=[[-1, 16], [0, 128]],
                                               compare_op=ALU.is_equal, fill=0.0, base=0,
                                               channel_multiplier=1),
             reads=["SEL"], writes=["SEL"])
        P.op("pool", I("iota", IOT[:], pattern=[[1, 16]], base=1, channel_multiplier=0),
             writes=["IOT"])
        P.op("dve", I("tensor_copy", out=SM1[:, 0:16], in_=IOT[:]), reads=["IOT"], writes=["SM1"])
        for g, w in enumerate(POOL_W):
            P.op("dve", I("tensor_scalar", out=INVT[:, g, :], in0=SM1[:, 0:16],
                                                           scalar1=float(w), scalar2=None, op0=ALU.min),
                 reads=["SM1"], writes=["INVT"])
        P.op("dve", I("reciprocal", out=INVT[:], in_=INVT[:]), reads=["INVT"], writes=["INVT"])

        ga = T0[0:120, 0:128]
        gb_ = T0[0:96, 128:256]
        for k, (src, r0) in enumerate([(g_mix_d, 0), (g_xa_d, 32), (g_mem_d, 64)]):
            P.dma("sp", I("dma_start",
                out=T0[r0:r0 + 32, 0:128], in_=src.rearrange("i (kc p) -> (i kc) p", p=128)), writes=[TK[0]])
        P.dma("sp", I("dma_start", out=T0[96:112, 0:128],
                                          in_=pscale_d.rearrange("j (kc p) -> (j kc) p", p=128)), writes=[TK[0]])
        P.dma("sp", I("dma_start", out=T0[112:120, 0:128],
                                          in_=g_fin_d.rearrange("(kc p) -> kc p", p=128)), writes=[TK[0]])
        P.dma("sp", I("dma_start", out=T0[0:96, 128:256],
                                          in_=conv_w_d.rearrange("j k (fc p) -> (j k fc) p", p=128)), writes=[TK[0]])
        P.group("pe", [I("transpose", out=PSA[:, 4, 0:120], in_=ga, identity=ident[0:120, 0:120]),
                       I("transpose", out=PSA[:, 4, 128:224], in_=gb_, identity=ident[0:96, 0:96])],
                reads=[TK[0], "ident"], writes=psk(4))
        P.op("dve", I("tensor_copy", out=GV[:, 0:120], in_=PSA[:, 4, 0:120]), reads=psk(4), writes=["GV"])
        P.op("dve", I("tensor_copy", out=GV[:, 120:216], in_=PSA[:, 4, 128:224]), reads=psk(4), writes=["GV"])

        def gcol(c):
            return GV[:, c:c + 1]

        wstate = {"slot": 0}

        def wload(src3):
            s = wstate["slot"]
            wstate["slot"] = (s + 1) % NW
            a, b = src3.shape[1], src3.shape[2]
            dst = WB[:, s, 0:a * b].rearrange("p (a b) -> p a b", a=a)
            P.dma("pool", I("dma_start", out=dst, in_=src3), writes=[("W", s)])
            return dst, ("W", s)

        def wcols(wd, r0, nk, c0, ncol):
            return wd[r0:r0 + nk * 128, c0:c0 + ncol].rearrange("(kc p) c -> p kc c", p=128)

        acc = {"b": 0}

        def next_acc():
            b = acc["b"]
            acc["b"] = (b + 1) % 8
            return b

        def proj(wv, wkey, col0, nk, src, srckey, evac, blks=range(5)):
            for bi in blks:
                t0, n = BLK[bi]
                b = next_acc()
                fns = []
                for kc in range(nk):
                    fns.append(I("matmul",
                        PSA[:, b, 0:n], lhsT=wv[:, kc, col0:col0 + 128], rhs=src[:, kc, t0:t0 + n],
                        start=(kc == 0), stop=(kc == nk - 1)))
                P.group("pe", fns, reads=[wkey] + [(srckey, kc, bi) for kc in range(nk)], writes=psk(b))
                evac(bi, b, n)

        SQK = [[("TB", 0, b_) for b_ in range(4)], [("TB", 0, 4)] + [("TB", 1, b_) for b_ in range(4)]]
        RSK = [("RS", 0), ("RS", 1)]
        nrm = {"k": 0}

        def rms_norm_fm(gbase, dst, dstkey, fp32_out=False):
            subs = []
            for bi in range(5):
                t0b, nb = BLK[bi]
                for t0 in range(t0b, t0b + nb, 256):
                    subs.append((bi, t0, min(256, t0b + nb - t0), nrm["k"] % 2))
                    nrm["k"] += 1

            def st_a(bi, t0, n, p):
                sq = TBf[:, p * 2048:p * 2048 + 8 * n].rearrange("p (k n) -> p k n", k=8)
                P.op("act", I("activation", out=sq, in_=XP[:, :, t0:t0 + n], func=AF.Square),
                     reads=[("XP", kc, bi) for kc in range(8)], writes=[SQK[p]])
                P.group("pe", [I("matmul", PSA[:, 4 + p, 0:n], lhsT=onesb[:], rhs=sq[:, kc, :],
                                 start=(kc == 0), stop=(kc == 7)) for kc in range(8)],
                        reads=[SQK[p], "onesb"], writes=psk(4 + p))

            def st_b(bi, t0, n, p):
                rs = RS[:, p * 256:p * 256 + n]
                P.op("act", I("activation", out=rs, in_=PSA[:, 4 + p, 0:n], func=AF.Ln, bias=EPSC[:, 0:1], scale=1.0 / D),
                     reads=psk(4 + p) + ["EPSC"], writes=[RSK[p]])
                P.op("act", I("activation", out=rs, in_=rs, func=AF.Exp, scale=-0.5), reads=[RSK[p]], writes=[RSK[p]])
                for kc in range(8):
                    P.op("dve", I("scalar_tensor_tensor", out=dst(kc, t0, n), in0=XP[:, kc, t0:t0 + n],
                                  scalar=gcol(gbase + kc), in1=rs, op0=ALU.mult, op1=ALU.mult),
                         reads=[("XP", kc, bi), RSK[p], "GV"], writes=[(dstkey, kc, bi)])

            st_a(*subs[0])
            for k in range(1, len(subs)):
                st_a(*subs[k])
                st_b(*subs[k - 1])
            st_b(*subs[-1])

        EPSC = sb("EPSC", [128, 1], F32)
        P.op("pool", I("memset", EPSC[:], EPS), writes=["EPSC"])

        def resid_add(fc, srckey_unused=None):
            def evac(bi, b, n):
                t0 = BLK[bi][0]
                P.op("dve", I("tensor_tensor", out=XP[:, fc, t0:t0 + n], in0=PSA[:, b, 0:n],
                                                      in1=XP[:, fc, t0:t0 + n], op=ALU.add),
                     reads=psk(b) + [("XP", fc, bi)], writes=[("XP", fc, bi)])
            return evac

        def out_proj(wd, r0, src, srckey, hooks=None):
            for pr in range(4):
                if hooks and pr in hooks:
                    hooks[pr]()
                wv, wkey = wload(wcols(wd, r0, 8, pr * 256, 256))
                for jj in range(2):
                    fc = pr * 2 + jj
                    proj(wv, wkey, jj * 128, 8, src, srckey, resid_add(fc))

        for q8 in range(8):
            xst = TT[:, q8 % 2, 0:2048].rearrange("p (a b) -> p a b", a=2)
            xk = TK[q8 % 2]
            P.dma("sp", I("dma_start", out=xst, in_=xp_d[q8 * 256:(q8 + 1) * 256, :].rearrange("(t p) f -> p t f", p=128)),
                  writes=[xk])
            for kc in range(8):
                b = next_acc()
                P.group("pe", [I("transpose", out=PSA[:, b, t * 128:(t + 1) * 128], in_=xst[:, t, kc * 128:(kc + 1) * 128],
                                 identity=ident[:]) for t in range(2)], reads=[xk, "ident"], writes=psk(b))
                if kc % 2:
                    P.op("act", I("activation", out=XP[:, kc, q8 * 256:(q8 + 1) * 256], in_=PSA[:, b, 0:256], func=AF.Copy),
                         reads=psk(b), writes=[("XP", kc, q8 // 2)])
                else:
                    P.op("dve", I("tensor_copy", out=XP[:, kc, q8 * 256:(q8 + 1) * 256], in_=PSA[:, b, 0:256]),
                         reads=psk(b), writes=[("XP", kc, q8 // 2)])
        P.dma("sp", I("dma_start", out=OST[0:16, :], in_=xs_d), writes=["OST"])
        b = next_acc()
        P.group("pe", [I("transpose", out=PSA[:, b, kc * 16:(kc + 1) * 16],
                                                         in_=OST[0:16, kc * 128:(kc + 1) * 128],
                                                         identity=ident[0:16, 0:16]) for kc in range(8)],
                reads=["OST", "ident"], writes=psk(b))
        P.op("dve", I("tensor_copy", out=XP[:, :, 2048:2064],
                                                 in_=PSA[:, b, 0:128].rearrange("p (k s) -> p k s", k=8)),
             reads=psk(b), writes=[("XP", kc, 4) for kc in range(8)])

        def even_layer(i):
            j = i // 2
            wab = w_in_ab_d[j]
            P.dma("sp", I("dma_start", out=GVT, in_=sgu_g_d[j].partition_broadcast(128)), writes=[TK[2]])
            for rep in range(2):
                P.dma("sp", I("dma_start",
                    out=BS8.rearrange("p (g two) t -> p g two t", two=2)[:, :, rep, :],
                    in_=sgu_b_d[j].rearrange("g t -> (g t)").partition_broadcast(128).rearrange("p (g t) -> p g t", g=4)),
                    writes=[TK[2]])
            P.dma("sp", I("dma_start", out=WSC[:], in_=sgu_w_d[j][:, 0, 0].partition_broadcast(16), allow_slow_non_contiguous=True),
                  writes=["WSC"])
            wst = T1[:, 0:512].rearrange("p (g s) -> p g s", g=4)
            P.dma("sp", I("dma_start", out=wst, in_=sgu_w_d[j].rearrange("g t s -> t g s")), writes=[TK[1]])
            P.group("pe", [I("transpose", out=PSA[:, 5, g * 128:(g + 1) * 128], in_=wst[:, g, :],
                                                      identity=ident[:]) for g in range(4)],
                    reads=[TK[1], "ident"], writes=psk(5))
            P.op("dve", I("tensor_copy", out=T1[:, 512:1024], in_=PSA[:, 5, :]), reads=psk(5), writes=[TK[1]])
            P.op("pool", I("affine_select", out=WT[:], in_=T1[:, 512:1024].rearrange("p (g t) -> p g t", g=4),
                                                   pattern=[[0, 4], [1, 128]], compare_op=ALU.is_ge, fill=0.0,
                                                   base=0, channel_multiplier=-1),
                 reads=[TK[1]], writes=["WT"])
            for g in range(4):
                P.op("dve", I("tensor_scalar", out=DG[:, g, :], in0=ident[0:16, 0:16],
                                                           scalar1=WSC[:, g:g + 1], scalar2=None, op0=ALU.mult),
                     reads=["WSC", "ident"], writes=["DG"])
            spt = T0[0:120, 0:2048].rearrange("p (a c) -> p a c", a=2)
            P.dma("sp", I("dma_start", out=spt, in_=spool_d[j].rearrange("(a s) r c -> (s r) a c", a=2)),
                  writes=[TK[0]])
            out_toks.append(P.dma("sp", I("dma_start", out=nps_d[j][:, 0:14, :], in_=spool_d[j][:, 1:15, :])))
            for a in range(2):
                for cc in range(8):
                    b = next_acc()
                    P.op("pe", I("transpose", out=PSA[:, b, 0:120],
                                                                       in_=spt[:, a, cc * 128:(cc + 1) * 128],
                                                                       identity=ident[0:120, 0:120]),
                         reads=[TK[0], "ident"], writes=psk(b))
                    w = POOL_W[cc // 2]
                    P.op("dve", I("tensor_reduce",
                        out=CSUM[:, cc, a * 8:(a + 1) * 8],
                        in_=PSA[:, b, 0:120].rearrange("p (s r) -> p s r", s=8)[:, :, 16 - w:15],
                        axis=AX.X, op=ALU.add),
                        reads=psk(b), writes=["CSUM"])

            rms_norm_fm(i * 8, lambda kc, t0, n: H[:, kc, t0:t0 + n], "H")

            wvs = [wload(wcols(wab, 0, 8, 3 * D + q * 256, 256)) for q in range(4)]
            jk = TT[:, 0, :].bitcast(BF16)[:, 0:1024].rearrange("p (a b) -> p a b", a=2)

            def e1_cfg(ti):
                t0, m = (ti * 128, 128) if ti < 16 else (2048, NS)
                bi = ti // 4 if ti < 16 else 4
                par = ti % 2
                vb = (4, 5) if par == 0 else (0, 1)
                mb = (6, 7) if par == 0 else (2, 3)
                vnk = [("TB", par, b_) for b_ in range(5)]
                return t0, m, bi, par, vb, mb, vnk, VST[0:m, par:par + 1], ("VST", par)

            def e1_a(ti):
                t0, m, bi, par, vb, mb, vnk, vst, vstk = e1_cfg(ti)
                fns = []
                for q in range(4):
                    for kc in range(8):
                        fns.append(I("matmul", PSA[0:m, vb[q // 2], (q % 2) * 256:(q % 2) * 256 + 256],
                                     lhsT=H[:, kc, t0:t0 + m], rhs=wvs[q][0][:, kc, :], start=(kc == 0), stop=(kc == 7)))
                P.group("pe", fns, reads=[k for _, k in wvs] + [("H", kc, bi) for kc in range(8)], writes=psk(*vb))
                vps = PSA[0:m, vb[0]:vb[0] + 2, :]
                vn = TB[0:m, par, 0:1024]
                P.op("act", I("activation", out=jk[0:m], in_=vps, func=AF.Square, accum_out=vst),
                     reads=psk(*vb), writes=[TK[0], vstk])
                P.op("act", I("activation", out=vst, in_=vst, func=AF.Sqrt, bias=EPSC[0:m, 0:1], scale=1.0 / D),
                     reads=[vstk, "EPSC"], writes=[vstk])
                P.op("dve", I("reciprocal", out=vst, in_=vst), reads=[vstk], writes=[vstk])
                P.op("dve", I("scalar_tensor_tensor", out=vn.rearrange("p (a b) -> p a b", a=2), in0=vps, scalar=vst,
                              in1=GVT[0:m, :].rearrange("p (a b) -> p a b", a=2), op0=ALU.mult, op1=ALU.mult),
                     reads=psk(*vb) + [vstk, TK[2]], writes=[vnk])
                if ti == 16:
                    P.op("dve", I("scalar_tensor_tensor", out=OST[0:16, :].rearrange("p (a b) -> p a b", a=2), in0=vps,
                                  scalar=vst, in1=GVT[0:16, :].rearrange("p (a b) -> p a b", a=2),
                                  op0=ALU.mult, op1=ALU.mult),
                         reads=psk(*vb) + [vstk, TK[2]], writes=["OST"])
                    out_toks.append(P.dma("sp", I("dma_start", out=nsv_d[j], in_=OST[0:16, :]), reads=["OST"]))

            def e1_b(ti):
                t0, m, bi, par, vb, mb, vnk, vst, vstk = e1_cfg(ti)
                vn = TB[0:m, par, 0:1024]
                if ti == 16:
                    fns = [I("matmul", PSA[:, mb[0], cc * 16:(cc + 1) * 16], lhsT=vn[:, cc * 128:(cc + 1) * 128],
                             rhs=DG[:, cc // 2, :], start=True, stop=True) for cc in range(8)]
                    P.group("pe", fns, reads=[vnk, "DG"], writes=psk(mb[0]))
                    for cc in range(8):
                        P.op("dve", I("tensor_scalar", out=YA[:, cc, 2048:2064], in0=PSA[:, mb[0], cc * 16:(cc + 1) * 16],
                                      scalar1=BS8[:, cc, 0:1], scalar2=None, op0=ALU.add),
                             reads=psk(mb[0]) + [TK[2]], writes=[("YA", cc, 4)])
                else:
                    fns = [I("matmul", PSA[:, mb[cc // 4], (cc % 4) * 128:(cc % 4) * 128 + 128],
                             lhsT=vn[:, cc * 128:(cc + 1) * 128], rhs=WT[:, cc // 2, :], start=True, stop=True)
                           for cc in range(8)]
                    P.group("pe", fns, reads=[vnk, "WT"], writes=psk(*mb))
                    P.op("dve", I("tensor_tensor", out=YA[:, :, t0:t0 + 128],
                                  in0=PSA[:, mb[0]:mb[0] + 2, :].rearrange("p a (c t) -> p (a c) t", c=4),
                                  in1=BS8, op=ALU.add),
                         reads=psk(*mb) + [TK[2]], writes=[("YA", cc, bi) for cc in range(8)])

            e1_a(0)
            for ti in range(1, 17):
                e1_a(ti)
                e1_b(ti - 1)
            e1_b(16)

            for pr in range(4):
                wg, wgk = wload(wcols(wab, 0, 8, 4 * D + pr * 256, 256))
                wu, wuk = wload(wcols(wab, 0, 8, 2 * D + pr * 256, 256))
                for jj in range(2):
                    def ev_gb(bi, b, n, jj=jj):
                        t0 = BLK[bi][0]
                        P.op("act", I("activation", out=TB[:, jj, t0:t0 + n], in_=PSA[:, b, 0:n], func=AF.Silu),
                             reads=psk(b), writes=[("TB", jj, bi)])
                    proj(wg, wgk, jj * 128, 8, H, "H", ev_gb)
                for jj in range(2):
                    cc = pr * 2 + jj

                    def ev_u(bi, b, n, jj=jj, cc=cc):
                        t0 = BLK[bi][0]
                        P.op("dve", I("tensor_tensor", out=YA[:, cc, t0:t0 + n], in0=PSA[:, b, 0:n],
                                                              in1=YA[:, cc, t0:t0 + n], op=ALU.mult),
                             reads=psk(b) + [("YA", cc, bi)], writes=[("YA", cc, bi)])
                        P.op("dve", I("tensor_tensor", out=YA[:, cc, t0:t0 + n], in0=YA[:, cc, t0:t0 + n],
                                                              in1=TB[:, jj, t0:t0 + n], op=ALU.mult),
                             reads=[("YA", cc, bi), ("TB", jj, bi)], writes=[("YA", cc, bi)])
                    proj(wu, wuk, jj * 128, 8, H, "H", ev_u)
            out_proj(w_out_ab_d[j], D, YA, "YA",
                     {0: (lambda: att_mem_pre(0)), 2: (lambda: att_mem_pe(0))} if (i == 0 and DBG["att"]) else None)

            for k in range(3):
                P.op("dve", I("memset", TT[:, k, 0:16], 0.0), writes=[TK[k]])
            for g in range(4):
                w = POOL_W[g]
                wx, wxk = wload(wcols(wab, 0, 8, g * 256, 256))
                wga, wgak = wload(wcols(wab, 0, 8, D + g * 256, 256))
                pm, pmk = wload(pmaps_d[j, g].rearrange("(kc p) d -> p kc d", p=128))
                for jj in range(2):
                    cc = g * 2 + jj

                    def ev_xa(bi, b, n, cc=cc):
                        t0 = BLK[bi][0]
                        if bi == 4:
                            P.op("act", I("activation", out=XAS[:, cc, :], in_=PSA[:, b, 0:n], func=AF.Copy),
                                 reads=psk(b), writes=["XAS"])
                        else:
                            P.op("act", I("activation", out=T0[:, 16 + t0:16 + t0 + n], in_=PSA[:, b, 0:n],
                                                               func=AF.Copy),
                                 reads=psk(b), writes=[TK[0]])
                            if bi == 3:
                                P.op("act", I("activation", out=XAL[:, cc, :], in_=PSA[:, b, 480:512],
                                                                   func=AF.Copy),
                                     reads=psk(b), writes=["XAL"])
                    proj(wx, wxk, jj * 128, 8, H, "H", ev_xa)
                    bufs = [T0, T1, T2]
                    keys = [TK[0], TK[1], TK[2]]
                    cur = 0
                    sh = 1
                    nxt = 1
                    while sh < w:
                        P.op("dve", I("tensor_tensor",
                            out=bufs[nxt][:, 16:2064], in0=bufs[cur][:, 16:2064], in1=bufs[cur][:, 16 - sh:2064 - sh],
                            op=ALU.add), reads=[keys[cur]], writes=[keys[nxt]])
                        cur = nxt
                        nxt = 2 if cur == 1 else 1
                        sh *= 2
                    S, Sk = bufs[cur], keys[cur]
                    P.op("dve", I("scalar_tensor_tensor",
                        out=TB[:, jj, 0:2048], in0=S[:, 16:2064], scalar=1.0 / w, in1=T0[:, 16:2064],
                        op0=ALU.mult, op1=ALU.subtract), reads=[Sk, TK[0]], writes=[("TB", jj, bi) for bi in range(4)])
                    P.op("dve", I("tensor_tensor", out=SM1[:, 0:16], in0=S[:, 16:32], in1=INVT[:, g, :],
                                                                  op=ALU.mult), reads=[Sk, "INVT"], writes=["SM1"])
                    P.op("dve", I("tensor_tensor", out=TB[:, jj, 0:16], in0=SM1[:, 0:16], in1=T0[:, 16:32],
                                                                 op=ALU.subtract),
                         reads=["SM1", TK[0]], writes=[("TB", jj, 0)])
                    P.op("dve", I("tensor_tensor", out=SM2[:, 0:16], in0=CSUM[:, cc, :], in1=XAS[:, cc, :],
                                                                 op=ALU.add), reads=["CSUM", "XAS"], writes=["SM2"])
                    P.op("dve", I("scalar_tensor_tensor",
                        out=TB[:, jj, 2048:2064], in0=SM2[:, 0:16], scalar=1.0 / w, in1=XAS[:, cc, :],
                        op0=ALU.mult, op1=ALU.subtract), reads=["SM2", "XAS"], writes=[("TB", jj, 4)])
                for jj in range(2):
                    cc = g * 2 + jj

                    def ev_ga(bi, b, n, cc=cc):
                        t0 = BLK[bi][0]
                        P.op("act", I("activation", out=YA[:, cc, t0:t0 + n], in_=PSA[:, b, 0:n], func=AF.Silu),
                             reads=psk(b), writes=[("YA", cc, bi)])
                    proj(wga, wgak, jj * 128, 8, H, "H", ev_ga)
                for jj in range(2):
                    cc = g * 2 + jj

                    def ev_ya(bi, b, n, cc=cc):
                        t0 = BLK[bi][0]
                        P.op("dve", I("scalar_tensor_tensor",
                            out=YA[:, cc, t0:t0 + n], in0=PSA[:, b, 0:n], scalar=gcol(96 + j * 8 + cc),
                            in1=YA[:, cc, t0:t0 + n], op0=ALU.mult, op1=ALU.mult),
                            reads=psk(b) + [("YA", cc, bi), "GV"], writes=[("YA", cc, bi)])
                    proj(pm, pmk, jj * 128, 2, TB, "TB", ev_ya)
            P.group("pe", [I("transpose", out=PSA[0:32, 4 + cc // 4, (cc % 4) * 128:(cc % 4) * 128 + 128],
                                                        in_=XAL[:, cc, :], identity=ident[:]) for cc in range(8)],
                    reads=["XAL", "ident"], writes=psk(4, 5))
            P.op("dve", I("tensor_copy", out=OST[:, :].rearrange("p (a b) -> p a b", a=2), in_=PSA[0:32, 4:6, :]),
                 reads=psk(4, 5), writes=["OST"])
            out_toks.append(P.dma("sp", I("dma_start", out=npp_d[j], in_=OST[17:32, :]), reads=["OST"]))
            P.group("pe", [I("transpose", out=PSA[0:16, 4 + cc // 4, (cc % 4) * 128:(cc % 4) * 128 + 128],
                                                        in_=XAS[:, cc, :], identity=ident[:]) for cc in range(8)],
                    reads=["XAS", "ident"], writes=psk(4, 5))
            P.op("dve", I("tensor_copy", out=OST[0:16, :].rearrange("p (a b) -> p a b", a=2), in_=PSA[0:16, 4:6, :]),
                 reads=psk(4, 5), writes=["OST"])
            out_toks.append(P.dma("sp", I("dma_start", out=nps_d[j][:, 14, :], in_=OST[0:16, :]), reads=["OST"]))
            out_proj(w_out_ab_d[j], 0, YA, "YA")

        def dummy_h():
            return None

        def odd_layer(i):
            j = i // 2
            wc = w_in_c_d[j]
            sct = T0[0:32, 0:2048]
            P.dma("sp", I("dma_start", out=sct, in_=sconv_d[j].rearrange("s r c -> (s r) c")), writes=[TK[0]])
            out_toks.append(P.dma("sp", I("dma_start", out=ncs_d[j][:, 0, :], in_=sconv_d[j][:, 1, :])))
            for q in range(4):
                P.group("pe", [I("transpose", out=PSA[:, 4, k * 32:(k + 1) * 32],
                                                              in_=sct[:, (q * 4 + k) * 128:(q * 4 + k + 1) * 128],
                                                              identity=ident[0:32, 0:32]) for k in range(4)],
                        reads=[TK[0], "ident"], writes=psk(4))
                P.op("dve", I("tensor_copy", out=SCX[:, q * 4:(q + 1) * 4, :],
                                                         in_=PSA[:, 4, 0:128].rearrange("p (k x) -> p k x", k=4)),
                     reads=psk(4), writes=["SCX"])
            rms_norm_fm(i * 8, lambda kc, t0, n: H[:, kc, t0:t0 + n], "H")
            P.op("dve", I("memset", T1[:, 0:16], 0.0), writes=[TK[1]])
            for half in range(2):
                for pr in range(4):
                    c0 = half * D + pr * 256
                    wcg, wcgk = wload(wcols(wc, 0, 8, 2 * D + c0, 256))
                    wxc, wxck = wload(wcols(wc, 0, 8, 4 * D + c0, 256))
                    wbg, wbgk = wload(wcols(wc, 0, 8, c0, 256))
                    wgg, wggk = wload(wcols(wc, 0, 8, 6 * D + c0, 256))
                    for jj in range(2):
                        fc = half * 8 + pr * 2 + jj
                        yc = fc % 8
                        cw = 120 + j * 48 + fc

                        def ev_cg(bi, b, n):
                            t0 = BLK[bi][0]
                            P.op("act", I("activation", out=T0[:, 16 + t0:16 + t0 + n], in_=PSA[:, b, 0:n],
                                                               func=AF.Copy), reads=psk(b), writes=[TK[0]])
                        proj(wcg, wcgk, jj * 128, 8, H, "H", ev_cg)

                        def ev_xc(bi, b, n, fc=fc):
                            t0 = BLK[bi][0]
                            P.op("dve", I("tensor_tensor", out=T1[:, 16 + t0:16 + t0 + n], in0=PSA[:, b, 0:n],
                                                                  in1=T0[:, 16 + t0:16 + t0 + n], op=ALU.mult),
                                 reads=psk(b) + [TK[0]], writes=[TK[1]])
                        proj(wxc, wxck, jj * 128, 8, H, "H", ev_xc)
                        P.op("act", I("activation", out=CXL[:, :, fc], in_=T1[:, 2062:2064], func=AF.Copy),
                             reads=[TK[1]], writes=["CXL"])
                        P.op("act", I("activation", out=CXS[:, fc, :], in_=T1[:, 2064:2080], func=AF.Copy),
                             reads=[TK[1]], writes=["CXS"])
                        P.op("act", I("activation", out=T2[:, 16:2064], in_=T1[:, 14:2062], func=AF.Copy,
                                                                  scale=gcol(cw)), reads=[TK[1], "GV"], writes=[TK[2]])
                        P.op("dve", I("scalar_tensor_tensor",
                            out=T2[:, 16:2064], in0=T1[:, 15:2063], scalar=gcol(cw + 16), in1=T2[:, 16:2064],
                            op0=ALU.mult, op1=ALU.add), reads=[TK[1], TK[2], "GV"], writes=[TK[2]])
                        P.op("dve", I("scalar_tensor_tensor",
                            out=T2[:, 16:2064], in0=T1[:, 16:2064], scalar=gcol(cw + 32), in1=T2[:, 16:2064],
                            op0=ALU.mult, op1=ALU.add), reads=[TK[1], TK[2], "GV"], writes=[TK[2]])
                        scx = SCX[:, fc, :].rearrange("p (s r) -> p s r", r=2)
                        P.op("dve", I("tensor_scalar",
                            out=T2[:, 2064:2080], in0=scx[:, :, 0], scalar1=gcol(cw), scalar2=None, op0=ALU.mult),
                            reads=["SCX", "GV"], writes=[TK[2]])
                        P.op("dve", I("scalar_tensor_tensor",
                            out=T2[:, 2064:2080], in0=scx[:, :, 1], scalar=gcol(cw + 16), in1=T2[:, 2064:2080],
                            op0=ALU.mult, op1=ALU.add), reads=["SCX", TK[2], "GV"], writes=[TK[2]])
                        P.op("dve", I("scalar_tensor_tensor",
                            out=T2[:, 2064:2080], in0=T1[:, 2064:2080], scalar=gcol(cw + 32), in1=T2[:, 2064:2080],
                            op0=ALU.mult, op1=ALU.add), reads=[TK[1], TK[2], "GV"], writes=[TK[2]])

                        def ev_bg(bi, b, n):
                            t0 = BLK[bi][0]
                            P.op("dve", I("tensor_tensor", out=T0[:, 16 + t0:16 + t0 + n], in0=PSA[:, b, 0:n],
                                                                  in1=T2[:, 16 + t0:16 + t0 + n], op=ALU.mult),
                                 reads=psk(b) + [TK[2]], writes=[TK[0]])
                        proj(wbg, wbgk, jj * 128, 8, H, "H", ev_bg)

                        def ev_g(bi, b, n, yc=yc):
                            t0 = BLK[bi][0]
                            P.op("act", I("activation", out=YA[:, yc, t0:t0 + n], in_=PSA[:, b, 0:n], func=AF.Silu),
                                 reads=psk(b), writes=[("YA", yc, bi)])
                            P.op("dve", I("tensor_tensor", out=YA[:, yc, t0:t0 + n], in0=YA[:, yc, t0:t0 + n],
                                                                  in1=T0[:, 16 + t0:16 + t0 + n], op=ALU.mult),
                                 reads=[("YA", yc, bi), TK[0]], writes=[("YA", yc, bi)])
                        proj(wgg, wggk, jj * 128, 8, H, "H", ev_g)
                out_proj(w_out_c_d[j], half * D, YA, "YA")
            P.op("pe", I("transpose", out=PSA[0:32, 4, 0:128], in_=CXL[:].rearrange("p r f -> p (r f)"),
                                             identity=ident[:]), reads=["CXL", "ident"], writes=psk(4))
            P.op("dve", I("tensor_copy", out=OST[0:32, 0:128], in_=PSA[0:32, 4, 0:128]), reads=psk(4), writes=["OST"])
            out_toks.append(P.dma("sp", I("dma_start",
                out=ncp_d[j].rearrange("r (fc c) -> (r fc) c", c=128), in_=OST[0:32, 0:128]), reads=["OST"]))
            for q in range(4):
                P.group("pe", [I("transpose", out=PSA[0:16, 4 + q, k * 128:(k + 1) * 128],
                                                              in_=CXS[:, q * 4 + k, :], identity=ident[:])
                               for k in range(4)], reads=["CXS", "ident"], writes=psk(4 + q))
            cst = T0[0:16, 0:2048]
            P.op("dve", I("tensor_copy", out=cst.rearrange("p (a b) -> p a b", a=4), in_=PSA[0:16, 4:8, :]),
                 reads=psk(4, 5, 6, 7), writes=[TK[0]])
            out_toks.append(P.dma("sp", I("dma_start", out=ncs_d[j][:, 1, :], in_=cst), reads=[TK[0]]))

        def att_mem_pre(i):
            mnt = TBf[:, 0:2048].rearrange("p (k m) -> p k m", k=8)
            memst = T2[:, 0:2048].rearrange("p (a b) -> p a b", a=2)
            P.dma("sp", I("dma_start", out=memst, in_=mem_d.rearrange("(a p) f -> p a f", p=128)), writes=[TK[2]])
            for a in range(2):
                P.op("act", I("activation", out=TB[:, 1, 0:1024], in_=memst[:, a, :], func=AF.Square,
                                                        accum_out=VST[:, a:a + 1]),
                     reads=[TK[2]], writes=[TBALL, [("VST", 0), ("VST", 1)]])
            P.op("act", I("activation", out=VST[:], in_=VST[:], func=AF.Sqrt, bias=EPSC[:, 0:1], scale=1.0 / D),
                 reads=[[("VST", 0), ("VST", 1)], "EPSC"], writes=[[("VST", 0), ("VST", 1)]])
            P.op("dve", I("reciprocal", out=VST[:], in_=VST[:]), reads=[[("VST", 0), ("VST", 1)]], writes=[[("VST", 0), ("VST", 1)]])
            for a in range(2):
                P.op("dve", I("tensor_scalar", out=memst[:, a, :], in0=memst[:, a, :], scalar1=VST[:, a:a + 1],
                                                           scalar2=None, op0=ALU.mult),
                     reads=[TK[2], [("VST", 0), ("VST", 1)]], writes=[TK[2]])

        def att_mem_pe(i):
            mnt = TBf[:, 0:2048].rearrange("p (k m) -> p k m", k=8)
            memst = T2[:, 0:2048].rearrange("p (a b) -> p a b", a=2)
            for kc in range(8):
                b = next_acc()
                P.group("pe", [I("transpose", out=PSA[:, b, a * 128:(a + 1) * 128],
                                                                      in_=memst[:, a, kc * 128:(kc + 1) * 128],
                                                                      identity=ident[:]) for a in range(2)],
                        reads=[TK[2], "ident"], writes=psk(b))
                P.op("dve", I("tensor_scalar", out=mnt[:, kc, :], in0=PSA[:, b, 0:256],
                                                                  scalar1=gcol(64 + i * 8 + kc), scalar2=None, op0=ALU.mult),
                     reads=psk(b) + ["GV"], writes=[TBALL])
            for which, (wd, od) in enumerate([(wk_d, nmk_d), (wv_d, nmv_d)]):
                for q in range(4):
                    wv_, wk_ = wload(wcols(wd[i], 0, 8, q * 256, 256))
                    for a in range(2):
                        b = next_acc()
                        P.group("pe", [I("matmul",
                            PSA[:, b, 0:256], lhsT=mnt[:, kc, a * 128:(a + 1) * 128], rhs=wv_[:, kc, :],
                            start=(kc == 0), stop=(kc == 7)) for kc in range(8)],
                            reads=[TBALL, wk_], writes=psk(b))
                        stg = T0[:, (a * 4 + q) * 256:(a * 4 + q + 1) * 256]
                        P.op("act", I("activation", out=stg, in_=PSA[:, b, 0:256], func=AF.Copy),
                             reads=psk(b), writes=[TK[0]])
                        if which == 1:
                            P.op("dve", I("tensor_copy", out=VM[:, a, q * 256:(q + 1) * 256],
                                                                              in_=PSA[:, b, 0:256]),
                                 reads=psk(b), writes=["VM"])
                    if which == 0:
                        for jj in range(2):
                            b = next_acc()
                            dc = q * 2 + jj
                            P.group("pe", [I("matmul",
                                PSA[:, b, 0:256], lhsT=wv_[:, kc, jj * 128:(jj + 1) * 128], rhs=mnt[:, kc, :],
                                start=(kc == 0), stop=(kc == 7)) for kc in range(8)],
                                reads=[TBALL, wk_], writes=psk(b))
                            P.op("act", I("activation", out=KT[:, dc, :], in_=PSA[:, b, 0:256],
                                                                          func=AF.Copy), reads=psk(b), writes=["KT"])
                for a in range(2):
                    out_toks.append(P.dma("sp", I("dma_start", out=od[i][a * 128:(a + 1) * 128, :],
                                                  in_=T0[:, a * 1024:(a + 1) * 1024]), reads=[TK[0]]))


        def attention(i):
            P.mute = not DBG.get("a_q", True)
            rms_norm_fm(32 + i * 8, lambda kc, t0, n: H[:, kc, t0:t0 + n], "H")

            for pr in range(4):
                wq, wqk = wload(wcols(wq_d[i], 0, 8, pr * 256, 256))
                for jj in range(2):
                    dc = pr * 2 + jj

                    def ev_q(bi, b, n, dc=dc):
                        t0 = BLK[bi][0]
                        P.op("act", I("activation", out=YA[:, dc, t0:t0 + n], in_=PSA[:, b, 0:n], func=AF.Copy,
                                                           scale=0.0625), reads=psk(b), writes=[("YA", dc, bi)])
                    proj(wq, wqk, jj * 128, 8, H, "H", ev_q, blks=range(4))
                b = next_acc()
                P.group("pe", [I("matmul", PSA[0:16, b, 0:256], lhsT=H[:, kc, 2048:2064],
                                                                     rhs=wq[:, kc, :], start=(kc == 0), stop=(kc == 7))
                               for kc in range(8)], reads=[wqk] + [("H", kc, 4) for kc in range(8)], writes=psk(b))
                P.op("act", I("activation", out=QS[:, pr * 256:(pr + 1) * 256], in_=PSA[0:16, b, 0:256],
                                                               func=AF.Copy, scale=0.0625), reads=psk(b), writes=["OST"])

            P.mute = not DBG.get("a_core", True)
            kvb = [[TT[:, a_, :].bitcast(BF16)[:, k * 2048:(k + 1) * 2048].rearrange("p (a c) -> p a c", a=2)
                    for k in range(2)] for a_ in range(2)]
            ex = TB[:, 1, 0:1024].rearrange("p (a b) -> p a b", a=2)
            EXK = [("TB", 1, 0), ("TB", 1, 1)]

            def sample_pv(s):
                par = s % 2
                vsb = kvb[par][1]
                for dc in range(8):
                    P.group("pe", [I("matmul", PSA[:, 5, dc * 16 + s:dc * 16 + s + 1],
                                     lhsT=vsb[:, mt, dc * 128:(dc + 1) * 128],
                                     rhs=ES[:, s, mt, dc // 2:dc // 2 + 1], start=(mt == 0), stop=(mt == 1))
                                   for mt in range(2)],
                            reads=[TK[par][1], ("ES", s)], writes=psk(5))

            n_it = 0
            for bi in range(4):
                t0 = BLK[bi][0]
                for h in range(4):
                    s = n_it
                    par = s % 2
                    n_it += 1
                    ksb, vsb = kvb[par][0], kvb[par][1]
                    P.dma("pool", I("dma_start", out=ksb, in_=ck_d[i, s].rearrange("(a p) c -> p a c", p=128)),
                          writes=[TK[par][0]])
                    P.dma("pool", I("dma_start", out=vsb, in_=cv_d[i, s].rearrange("(a p) c -> p a c", p=128)),
                          writes=[TK[par][1]])
                    for mt in range(2):
                        P.group("pe", [I("matmul", PSA[:, mt, :], lhsT=KT[:, 2 * h + dcc, mt * 128:(mt + 1) * 128],
                                         rhs=YA[:, 2 * h + dcc, t0:t0 + 512], start=(dcc == 0), stop=(dcc == 1))
                                       for dcc in range(2)],
                                reads=["KT", ("YA", 2 * h, bi), ("YA", 2 * h + 1, bi)], writes=psk(mt))
                    for hf in range(2):
                        P.op("pe", I("matmul", PSA[:, 2 + hf, :], lhsT=SEL[:, s, :], rhs=QS[:, hf * 512:(hf + 1) * 512],
                                     start=True, stop=True), reads=["SEL", "OST"], writes=psk(2 + hf))
                    if s > 0:
                        sample_pv(s - 1)
                    P.op("act", I("activation", out=ex, in_=PSA[:, 0:2, :], func=AF.Exp), reads=psk(0, 1), writes=[EXK])
                    for hf in range(2):
                        for mt in range(2):
                            for hh in range(2):
                                hd = hf * 2 + hh
                                P.op("dve", I("scalar_tensor_tensor", out=TB[:, 0, 0:256],
                                              in0=ksb[:, mt, hd * 256:(hd + 1) * 256], scalar=1.0,
                                              in1=PSA[:, 2 + hf, hh * 256:hh * 256 + 256], op0=ALU.mult, op1=ALU.mult,
                                              accum_out=SC[:, s, mt, hd:hd + 1]),
                                     reads=[TK[par][0]] + psk(2 + hf), writes=[("TB", 0, 0), ("SC", s)])
                    P.group("pe", [I("matmul", PSA[:, 4, :], lhsT=onesb[:], rhs=ex[:, mt, :],
                                     start=(mt == 0), stop=(mt == 1)) for mt in range(2)],
                            reads=[EXK, "onesb"], writes=psk(4))
                    for dcc in range(2):
                        P.group("pe", [I("matmul", PSA[:, 6 + dcc, :], lhsT=VM[:, mt, (2 * h + dcc) * 128:(2 * h + dcc + 1) * 128],
                                         rhs=ex[:, mt, :], start=(mt == 0), stop=(mt == 1)) for mt in range(2)],
                                reads=[EXK, "VM"], writes=psk(6 + dcc))
                    P.op("act", I("activation", out=RS[:], in_=PSA[:, 4, :], func=AF.Ln), reads=psk(4), writes=[RSK])
                    P.op("act", I("activation", out=RS[:], in_=RS[:], func=AF.Exp, scale=-1.0), reads=[RSK], writes=[RSK])
                    for dcc in range(2):
                        P.op("dve", I("tensor_tensor", out=YA[:, 2 * h + dcc, t0:t0 + 512],
                                                                           in0=PSA[:, 6 + dcc, :], in1=RS[:], op=ALU.mult),
                             reads=psk(6 + dcc) + [RSK], writes=[("YA", 2 * h + dcc, bi)])
                    P.op("act", I("activation", out=ES[:, s], in_=SC[:, s], func=AF.Exp),
                         reads=[("SC", s)], writes=[("ES", s)])
            sample_pv(NS - 1)
            P.mute = not DBG.get("a_samp", True)
            P.op("pe", I("matmul", PSA[:, 4, 0:128], lhsT=onesb[:], rhs=ES[:].rearrange("p s m h -> p (s m h)"),
                                          start=True, stop=True), reads=[("ES", s) for s in range(NS)] + ["onesb"],
                 writes=psk(4))
            dv = PSA[:, 4, 0:128].rearrange("p (s m h) -> p s m h", s=16, m=2)
            P.op("dve", I("tensor_copy", out=DS[:], in_=dv[:, :, 0, :]), reads=psk(4), writes=["DS"])
            P.op("dve", I("tensor_tensor", out=DS[:], in0=dv[:, :, 1, :], in1=DS[:], op=ALU.add),
                 reads=psk(4) + ["DS"], writes=["DS"])
            P.op("dve", I("reciprocal", out=DS[:], in_=DS[:]), reads=["DS"], writes=["DS"])
            for dc in range(8):
                P.op("dve", I("tensor_tensor", out=YA[:, dc, 2048:2064], in0=PSA[:, 5, dc * 16:(dc + 1) * 16],
                                                             in1=DS[:, :, dc // 2], op=ALU.mult),
                     reads=psk(5) + ["DS"], writes=[("YA", dc, 4)])
            P.mute = not DBG.get("a_out", True)
            hooks = None
            if i + 1 < DBG["layers"]:
                hooks = {0: (lambda: att_mem_pre(i + 1)), 2: (lambda: att_mem_pe(i + 1))}
            out_proj(wo_d[i], 0, YA, "YA", hooks)
            P.mute = False

        for i in range(DBG["layers"]):
            if DBG["mix"]:
                if i % 2 == 0:
                    even_layer(i)
                else:
                    odd_layer(i)
            if DBG["att"]:
                attention(i)

        fsubs = []
        kk = 0
        for bi in range(5):
            t0b, nb = BLK[bi]
            for t0 in range(t0b, t0b + nb, 256):
                fsubs.append((bi, t0, min(256, t0b + nb - t0), kk % 2))
                kk += 1
        otl = {"k": 0}

        def fin_a(bi, t0, n, p):
            sq = TBf[:, p * 2048:p * 2048 + 8 * n].rearrange("p (k n) -> p k n", k=8)
            P.op("act", I("activation", out=sq, in_=XP[:, :, t0:t0 + n], func=AF.Square),
                 reads=[("XP", kc, bi) for kc in range(8)], writes=[SQK[p]])
            P.group("pe", [I("matmul", PSA[:, 4 + p, 0:n], lhsT=onesb[:], rhs=sq[:, kc, :],
                             start=(kc == 0), stop=(kc == 7)) for kc in range(8)],
                    reads=[SQK[p], "onesb"], writes=psk(4 + p))

        def fin_b(bi, t0, n, p):
            rs = RS[:, p * 256:p * 256 + n]
            yst = TT[:, p, 0:2048].rearrange("p (k n) -> p k n", k=8)
            P.op("act", I("activation", out=rs, in_=PSA[:, 4 + p, 0:n], func=AF.Ln, bias=EPSC[:, 0:1], scale=1.0 / D),
                 reads=psk(4 + p) + ["EPSC"], writes=[RSK[p]])
            P.op("act", I("activation", out=rs, in_=rs, func=AF.Exp, scale=-0.5), reads=[RSK[p]], writes=[RSK[p]])
            for kc in range(8):
                P.op("dve", I("scalar_tensor_tensor", out=yst[:, kc, 0:n], in0=XP[:, kc, t0:t0 + n],
                              scalar=gcol(112 + kc), in1=rs, op0=ALU.mult, op1=ALU.mult),
                     reads=[("XP", kc, bi), RSK[p], "GV"], writes=[TK[p]])

        def fin_c(bi, t0, n, p):
            yst = TT[:, p, 0:2048].rearrange("p (k n) -> p k n", k=8)
            m = 128 if bi < 4 else NS
            for tl in range((n + 127) // 128):
                o = otl["k"] % 2
                otl["k"] += 1
                for hf in range(2):
                    b = (0, 1, 2, 3, 6, 7)[(otl["k"] * 2 + hf) % 6]
                    P.group("pe", [I("transpose", out=PSA[0:m, b, k * 128:(k + 1) * 128],
                                     in_=yst[:, hf * 4 + k, tl * 128:tl * 128 + m], identity=ident[:])
                                   for k in range(4)], reads=[TK[p], "ident"], writes=psk(b))
                    o0 = o * 1024 + hf * 512
                    if hf:
                        P.op("act", I("activation", out=T2[0:m, o0:o0 + 512], in_=PSA[0:m, b, :], func=AF.Copy),
                             reads=psk(b), writes=[TK[2][o]])
                    else:
                        P.op("dve", I("tensor_copy", out=T2[0:m, o0:o0 + 512], in_=PSA[0:m, b, :]),
                             reads=psk(b), writes=[TK[2][o]])
                if bi < 4:
                    r0 = t0 + tl * 128
                    out_toks.append(P.dma("sp", I("dma_start", out=yp_d[r0:r0 + 128, :], in_=T2[:, o * 1024:o * 1024 + 1024]),
                                          reads=[TK[2][o]]))
                else:
                    out_toks.append(P.dma("sp", I("dma_start", out=ys_d, in_=T2[0:16, o * 1024:o * 1024 + 1024]),
                                          reads=[TK[2][o]]))

        fin_a(*fsubs[0])
        for k in range(1, len(fsubs)):
            fin_a(*fsubs[k])
            fin_b(*fsubs[k - 1])
            fin_c(*fsubs[k - 1])
        fin_b(*fsubs[-1])
        fin_c(*fsubs[-1])

        P.wait("sp", out_toks)
        P.replay(st)
    return nc


_CACHE = {}

W_NAMES = ["norm_mix_g", "norm_xattn_g", "norm_mem_g", "w_in_ab", "pool_maps", "pool_scale", "sgu_w", "sgu_b",
           "sgu_g", "w_out_ab", "w_in_c", "conv_w", "w_out_c", "w_q", "w_k", "w_v", "w_o", "norm_final_g"]


def kernel(**inputs):
    f = lambda a: np.ascontiguousarray(np.asarray(a, dtype=np.float32))
    if "nc" not in _CACHE:
        _CACHE["nc"] = build_program()
    nc = _CACHE["nc"]
    shared = {k: f(inputs[k]) for k in W_NAMES}
    xp = np.asarray(inputs["x_prompt"]); xs = np.asarray(inputs["x_sample"])
    mem = np.asarray(inputs["mem_prompt"]); sp = np.asarray(inputs["state_pool"])
    scv = np.asarray(inputs["state_conv"]); ck = np.asarray(inputs["cache_mem_k"]); cv = np.asarray(inputs["cache_mem_v"])
    in_maps = []
    for c in range(NCORES):
        sl = slice(c * NS, (c + 1) * NS)
        m = dict(shared)
        m["xp"] = f(xp[c]); m["xs"] = f(xs[sl, 0]); m["mem"] = f(mem[c])
        m["spool"] = f(sp[:, sl]); m["sconv"] = f(scv[:, sl])
        m["ck"] = f(ck[:, sl].reshape(DEPTH, NS, 256, D)); m["cv"] = f(cv[:, sl].reshape(DEPTH, NS, 256, D))
        in_maps.append(m)
    res = run_bass_kernel_spmd(nc, in_maps, core_ids=list(range(NCORES)))
    R = res.results
    cat = lambda k, ax: np.concatenate([np.expand_dims(r[k], ax) if False else r[k] for r in R], axis=ax)
    y_prompt = np.stack([r["yp"] for r in R], 0)
    y_sample = np.concatenate([r["ys"] for r in R], 0)[:, None, :]
    npp = np.stack([r["npp"] for r in R], 1)
    nps = np.concatenate([r["nps"] for r in R], 1)
    ncp = np.stack([r["ncp"] for r in R], 1)
    ncs = np.concatenate([r["ncs"] for r in R], 1)
    nsv = np.concatenate([r["nsv"] for r in R], 1)[:, :, None, :]
    nmk = np.stack([r["nmk"] for r in R], 1).reshape(DEPTH, NCORES, 256, 4, 256)
    nmv = np.stack([r["nmv"] for r in R], 1).reshape(DEPTH, NCORES, 256, 4, 256)
    return (y_prompt, y_sample, npp, nps, ncp, ncs, nsv, nmk, nmv)
```
